# Optimizing a Trainium2 kernel written in Bass

```python
import math
import jax
import jax.numpy as jnp
from jax import lax
import numpy as np

D_MODEL = 1024
BATCH = 16
SEQ = 2048
DEPTH = 1

CHUNK = 64
N_META = 16
QBLK = 128

MLA_HEADS = 4
D_NOPE = 128
D_ROPE = 64
D_QK = D_NOPE + D_ROPE
D_V = 128
KV_RANK = 256
Q_RANK = 384
ROPE_THETA = 10000.0
MLA_WIDTH = MLA_HEADS * D_V

LRU_WIDTH = D_MODEL // 2
LRU_BLOCKS = 8
LRU_BLOCK = LRU_WIDTH // LRU_BLOCKS
CONV_W = 4
C_RGLRU = 8.0

MIX_WIDTH = MLA_WIDTH + LRU_WIDTH
IN_WIDTH = Q_RANK + KV_RANK + D_ROPE + LRU_WIDTH + LRU_WIDTH

D_FF = 2816
FFN_RESIDUAL = 0.5
EPS = 1e-6
NEG_INF = -1e30

kernel_name = "hymba_mla_rglru_macaron_block"


def _rmsnorm(x, g):
    xf = x.astype(jnp.float32)
    y = xf * lax.rsqrt(jnp.mean(xf * xf, axis=-1, keepdims=True) + EPS)
    return (y * g.astype(jnp.float32)).astype(x.dtype)


def _swiglu_half(h, g, w_gate, w_up, w_down):
    u = _rmsnorm(h, g)
    return FFN_RESIDUAL * ((jax.nn.silu(u @ w_gate) * (u @ w_up)) @ w_down)


def _rope(x, cos, sin):
    half = x.shape[-1] // 2
    x1, x2 = x[..., :half], x[..., half:]
    return jnp.concatenate([x1 * cos - x2 * sin, x2 * cos + x1 * sin], axis=-1)


def _mla(c_q, c_kv, k_r, q_latent_norm, w_uq, kv_latent_norm, w_uk, w_uv,
         q_head_norm, k_head_norm):
    B, L, _ = c_q.shape
    q = (_rmsnorm(c_q, q_latent_norm) @ w_uq).reshape(B, L, MLA_HEADS, D_QK)
    ckv = _rmsnorm(c_kv, kv_latent_norm)
    k_nope = (ckv @ w_uk).reshape(B, L, MLA_HEADS, D_NOPE)
    v = (ckv @ w_uv).reshape(B, L, MLA_HEADS, D_V)
    k_rope = jnp.broadcast_to(k_r[:, :, None, :], (B, L, MLA_HEADS, D_ROPE))
    k = jnp.concatenate([k_nope, k_rope], axis=-1)
    q = _rmsnorm(q, q_head_norm)
    k = _rmsnorm(k, k_head_norm)
    pos = jnp.arange(L, dtype=jnp.float32)
    inv_freq = ROPE_THETA ** (-jnp.arange(0, D_ROPE // 2, dtype=jnp.float32) / (D_ROPE // 2))
    ang = pos[:, None] * inv_freq[None, :]
    cos = jnp.cos(ang)[:, None, :].astype(q.dtype)
    sin = jnp.sin(ang)[:, None, :].astype(q.dtype)
    q = jnp.concatenate([q[..., :D_NOPE], _rope(q[..., D_NOPE:], cos, sin)], axis=-1)
    k = jnp.concatenate([k[..., :D_NOPE], _rope(k[..., D_NOPE:], cos, sin)], axis=-1)
    n_blk = -(-L // QBLK)
    L_pad = n_blk * QBLK
    padw = ((0, 0), (0, L_pad - L), (0, 0), (0, 0))
    q = jnp.pad(q, padw).transpose(0, 2, 1, 3)
    k = jnp.pad(k, padw).transpose(0, 2, 1, 3)
    v = jnp.pad(v, padw).transpose(0, 2, 1, 3)
    cid = (jnp.arange(L_pad, dtype=jnp.int32) + (CHUNK - N_META)) // CHUNK
    q_blocks = q.reshape(B, MLA_HEADS, n_blk, QBLK, D_QK).transpose(2, 0, 1, 3, 4)
    cid_blocks = cid.reshape(n_blk, QBLK)
    scale = 1.0 / math.sqrt(D_QK)

    def attend(args):
        qb, cq = args
        s = jnp.einsum('bhqd,bhkd->bhqk', qb, k,
                       preferred_element_type=jnp.float32) * scale
        mask = cid[None, :] <= cq[:, None]
        s = jnp.where(mask[None, None], s, NEG_INF)
        p = jax.nn.softmax(s, axis=-1).astype(v.dtype)
        return jnp.einsum('bhqk,bhkd->bhqd', p, v)

    o = lax.map(attend, (q_blocks, cid_blocks))
    o = o.transpose(1, 0, 3, 2, 4).reshape(B, L_pad, MLA_WIDTH)
    return o[:, :L]


def _rglru(u, gate, conv_w, conv_b, gate_a_w, gate_a_b, gate_x_w, gate_x_b, lru_lambda):
    B, L, W = u.shape
    xc = lax.conv_general_dilated(
        u, conv_w[:, None, :].astype(u.dtype), window_strides=(1,),
        padding=[(CONV_W - 1, 0)], dimension_numbers=('NWC', 'WIO', 'NWC'),
        feature_group_count=W) + conv_b
    xb = xc.reshape(B, L, LRU_BLOCKS, LRU_BLOCK)
    r = jax.nn.sigmoid(jnp.einsum('blni,nij->blnj', xb, gate_a_w).reshape(B, L, W) + gate_a_b)
    i = jax.nn.sigmoid(jnp.einsum('blni,nij->blnj', xb, gate_x_w).reshape(B, L, W) + gate_x_b)
    log_a = -C_RGLRU * r.astype(jnp.float32) * jax.nn.softplus(-lru_lambda.astype(jnp.float32))
    a = jnp.exp(log_a)
    mult = jnp.sqrt(-jnp.expm1(2.0 * log_a))
    first = (jnp.arange(L) == 0)[None, :, None]
    mult = jnp.where(first, 1.0, mult)
    b = mult * (i * xc).astype(jnp.float32)

    def combine(lhs, rhs):
        a1, b1 = lhs
        a2, b2 = rhs
        return a1 * a2, a2 * b1 + b2

    _, h = lax.associative_scan(combine, (a, b), axis=1)
    return h.astype(u.dtype) * jax.nn.gelu(gate)


def setup_inputs(seed: int = 0) -> dict:
    key = jax.random.key(seed)
    ks = iter(jax.random.split(key, 40))

    def dense(shape, fan_in):
        return jax.random.normal(next(ks), shape, jnp.float32) * (fan_in ** -0.5)

    def gain(n):
        return 1.0 + 0.05 * jax.random.normal(next(ks), (DEPTH, n), jnp.float32)

    def bias(n):
        return 0.01 * jax.random.normal(next(ks), (DEPTH, n), jnp.float32)

    x = jax.random.normal(next(ks), (BATCH, SEQ, D_MODEL), jnp.float32)
    meta_tokens = jax.random.normal(next(ks), (N_META, D_MODEL), jnp.float32)
    a0 = 0.9 + 0.099 * jax.random.uniform(next(ks), (DEPTH, LRU_WIDTH), jnp.float32)
    s0 = a0 ** (1.0 / C_RGLRU)
    lru_lambda = jnp.log(s0) - jnp.log1p(-s0)
    return {
        "x": x,
        "meta_tokens": meta_tokens,
        "ffn1_norm": gain(D_MODEL),
        "ffn1_w_gate": dense((DEPTH, D_MODEL, D_FF), D_MODEL),
        "ffn1_w_up": dense((DEPTH, D_MODEL, D_FF), D_MODEL),
        "ffn1_w_down": dense((DEPTH, D_FF, D_MODEL), D_FF),
        "mix_norm": gain(D_MODEL),
        "w_in": dense((DEPTH, D_MODEL, IN_WIDTH), D_MODEL),
        "q_latent_norm": gain(Q_RANK),
        "w_uq": dense((DEPTH, Q_RANK, MLA_HEADS * D_QK), Q_RANK),
        "kv_latent_norm": gain(KV_RANK),
        "w_uk": dense((DEPTH, KV_RANK, MLA_HEADS * D_NOPE), KV_RANK),
        "w_uv": dense((DEPTH, KV_RANK, MLA_HEADS * D_V), KV_RANK),
        "q_head_norm": gain(D_QK),
        "k_head_norm": gain(D_QK),
        "conv_w": dense((DEPTH, CONV_W, LRU_WIDTH), CONV_W),
        "conv_b": bias(LRU_WIDTH),
        "gate_a_w": dense((DEPTH, LRU_BLOCKS, LRU_BLOCK, LRU_BLOCK), LRU_BLOCK),
        "gate_a_b": bias(LRU_WIDTH),
        "gate_x_w": dense((DEPTH, LRU_BLOCKS, LRU_BLOCK, LRU_BLOCK), LRU_BLOCK),
        "gate_x_b": bias(LRU_WIDTH),
        "lru_lambda": lru_lambda,
        "attn_out_norm": gain(MLA_WIDTH),
        "lru_out_norm": gain(LRU_WIDTH),
        "w_out": dense((DEPTH, MIX_WIDTH, D_MODEL), MIX_WIDTH),
        "ffn2_norm": gain(D_MODEL),
        "ffn2_w_gate": dense((DEPTH, D_MODEL, D_FF), D_MODEL),
        "ffn2_w_up": dense((DEPTH, D_MODEL, D_FF), D_MODEL),
        "ffn2_w_down": dense((DEPTH, D_FF, D_MODEL), D_FF),
        "final_norm": gain(D_MODEL),
    }


def reference(x, meta_tokens, ffn1_norm, ffn1_w_gate, ffn1_w_up, ffn1_w_down,
              mix_norm, w_in, q_latent_norm, w_uq, kv_latent_norm, w_uk, w_uv,
              q_head_norm, k_head_norm, conv_w, conv_b, gate_a_w, gate_a_b,
              gate_x_w, gate_x_b, lru_lambda, attn_out_norm, lru_out_norm, w_out,
              ffn2_norm, ffn2_w_gate, ffn2_w_up, ffn2_w_down, final_norm):
    B = x.shape[0]
    meta = jnp.broadcast_to(meta_tokens.astype(x.dtype)[None], (B, N_META, D_MODEL))
    h = jnp.concatenate([meta, x], axis=1)
    o1 = Q_RANK
    o2 = o1 + KV_RANK
    o3 = o2 + D_ROPE
    o4 = o3 + LRU_WIDTH
    for l in range(DEPTH):
        h = h + _swiglu_half(h, ffn1_norm[l], ffn1_w_gate[l], ffn1_w_up[l], ffn1_w_down[l])
        z = _rmsnorm(h, mix_norm[l]) @ w_in[l]
        c_q, c_kv, k_r = z[..., :o1], z[..., o1:o2], z[..., o2:o3]
        u, g = z[..., o3:o4], z[..., o4:]
        y_mla = _mla(c_q, c_kv, k_r, q_latent_norm[l], w_uq[l], kv_latent_norm[l],
                     w_uk[l], w_uv[l], q_head_norm[l], k_head_norm[l])
        y_lru = _rglru(u, g, conv_w[l], conv_b[l], gate_a_w[l], gate_a_b[l],
                       gate_x_w[l], gate_x_b[l], lru_lambda[l])
        y = jnp.concatenate([_rmsnorm(y_mla, attn_out_norm[l]),
                             _rmsnorm(y_lru, lru_out_norm[l])], axis=-1)
        h = h + y @ w_out[l]
        h = h + _swiglu_half(h, ffn2_norm[l], ffn2_w_gate[l], ffn2_w_up[l], ffn2_w_down[l])
        h = _rmsnorm(h, final_norm[l])
    return h[:, N_META:]
```

```python
import bisect
import os
import types
from contextlib import ExitStack

import numpy as np
import concourse.bass as bass
import concourse.mybir as mybir
from concourse.bass_utils import run_bass_kernel_spmd

F32 = mybir.dt.float32
BF16 = mybir.dt.bfloat16
AF = mybir.ActivationFunctionType
ALU = mybir.AluOpType

D = 1024
DFF = 2816
NFC = 22
NFH = 11
SEQ = 2048
NMETA = 16
LTOT = SEQ + NMETA
NH = 4
EPS = 1e-6
SUBW = 512
MACW = NMETA + 2 * SUBW
GW = 520
NGRAN = 74
NKT = 17

C_FFN1 = 0; C_MIX = 8; C_FFN2 = 16; C_FIN = 24
C_QLAT = 32; C_KVLAT = 35
C_QN = 37; C_QR = 38; C_QRP = 39
C_KN = 40; C_KR = 41; C_KRP = 42
C_CW = 43
C_CB = 59; C_BA = 63; C_BX = 67; C_LAM = 71
C_AO = 75; C_LO = 79
NCONST = 83


class Buf:
    __slots__ = ("name", "w", "r", "rd", "x")

    def __init__(self, name, x=False):
        self.name = name
        self.x = x
        self.w = None
        self.r = {}
        self.rd = []


def _snap(fn):
    if fn is None or fn.__closure__ is None:
        return fn
    cells = []
    for c in fn.__closure__:
        try:
            v = c.cell_contents
        except ValueError:
            cells.append(c)
            continue
        if isinstance(v, types.FunctionType):
            v = _snap(v)
        cells.append(types.CellType(v))
    g = types.FunctionType(fn.__code__, fn.__globals__, fn.__name__, fn.__defaults__, tuple(cells))
    g.__kwdefaults__ = fn.__kwdefaults__
    return g


class Sched:
    ENG = ("pe", "act", "dve", "pool", "sp")
    KDMA = 8

    def __init__(self):
        self.ops = {e: [] for e in self.ENG}
        self.marks = {e: [] for e in self.ENG}
        self.waited = {e: {} for e in self.ENG}
        self.dma_n = {"sp": 0, "pool": 0, "act": 0}
        self.dma_cnt = {q: [0] * self.KDMA for q in self.dma_n}
        self.out_tokens = []

    def _resolve(self, tok):
        if tok[0] == "d":
            return ("d", tok[1], tok[2]), tok[3]
        eng, idx = tok[1], tok[2]
        marks = self.marks[eng]
        p = bisect.bisect_left(marks, idx)
        if p == len(marks):
            ops = self.ops[eng]
            j = len(ops) - 1
            while ops[j]["dmasem"] is not None or ops[j]["fn"] is None:
                j -= 1
            assert j >= idx
            ops[j]["mark"] = True
            marks.append(j)
            p = len(marks) - 1
        return ("e", eng), p + 1

    def _add_wait(self, eng, waits, tok):
        if tok is None:
            return
        if tok[0] == "e" and tok[1] == eng and eng == "pe":
            return
        key, val = self._resolve(tok)
        if self.waited[eng].get(key, 0) >= val:
            return
        self.waited[eng][key] = val
        waits.append((key, val))

    def op(self, eng, fn, reads=(), writes=(), dma_q=None):
        waits = []
        if any(b.x for b in reads):
            writes = list(writes) + [b for b in reads if b.x and b not in writes]
            reads = [b for b in reads if not b.x]
        for b in reads:
            self._add_wait(eng, waits, b.w)
        for b in writes:
            self._add_wait(eng, waits, b.w)
            for e2, i2 in b.r.items():
                self._add_wait(eng, waits, ("e", e2, i2))
            for t in b.rd:
                self._add_wait(eng, waits, t)
        idx = len(self.ops[eng])
        rec = {"waits": waits, "fn": _snap(fn), "mark": False, "dmasem": None}
        if dma_q is not None:
            q = eng
            k = self.dma_n[q] % self.KDMA
            self.dma_n[q] += 1
            if self.dma_cnt[q][k] > 0:
                self._add_wait(eng, waits, ("d", q, k, 16 * self.dma_cnt[q][k]))
            self.dma_cnt[q][k] += 1
            tok = ("d", q, k, 16 * self.dma_cnt[q][k])
            rec["dmasem"] = ("d", q, k)
        else:
            tok = ("e", eng, idx)
        self.ops[eng].append(rec)
        for b in writes:
            b.w = tok
            b.r = {}
            b.rd = []
        for b in reads:
            if b in writes:
                continue
            if tok[0] == "d":
                b.rd.append(tok)
            else:
                b.r[eng] = idx
        return tok

    def dma(self, q, out, in_, reads=(), writes=()):
        return self.op(q, lambda e: e.dma_start(out=out, in_=in_), reads, writes, dma_q=q)

    def finish(self):
        waits = []
        for t in self.out_tokens:
            self._add_wait("sp", waits, t)
        self.ops["sp"].append({"waits": waits, "fn": None, "mark": False, "dmasem": None})

    def sem_keys(self):
        keys = [("e", e) for e in self.ENG]
        for q in self.dma_n:
            for k in range(self.KDMA):
                keys.append(("d", q, k))
        return keys

    def emit(self, eng, e, sems):
        for rec in self.ops[eng]:
            for key, val in rec["waits"]:
                e.wait_ge(sems[key], val)
            if rec["fn"] is None:
                continue
            ins = rec["fn"](e)
            if rec["mark"]:
                ins.then_inc(sems[("e", eng)], 1)
            if rec["dmasem"] is not None:
                ins.then_inc(sems[rec["dmasem"]], 16)


class Slot:
    def __init__(self, kind, gi, ap, bufs):
        self.kind, self.gi, self.ap, self.bufs = kind, gi, ap, bufs


class Pool:
    def __init__(self, tensor, n):
        self.t = tensor
        self.n = n
        self.free = [True] * n
        self.bufs = [Buf(f"g{i}") for i in range(n)]

    def f(self):
        for i in range(0, self.n - 1, 2):
            if self.free[i] and self.free[i + 1]:
                self.free[i] = self.free[i + 1] = False
                ap = self.t[:, i * GW:(i + 2) * GW].bitcast(F32)
                return Slot("f", i, ap, [self.bufs[i], self.bufs[i + 1]])
        raise RuntimeError("pool exhausted (F)")

    def b(self):
        for i in range(self.n - 1, -1, -1):
            if self.free[i]:
                self.free[i] = False
                return Slot("b", i, self.t[:, i * GW:(i + 1) * GW], [self.bufs[i]])
        raise RuntimeError("pool exhausted (B)")

    def rel(self, *slots):
        for s in slots:
            if s.kind == "f":
                assert not self.free[s.gi] and not self.free[s.gi + 1]
                self.free[s.gi] = self.free[s.gi + 1] = True
            else:
                assert not self.free[s.gi]
                self.free[s.gi] = True


def build(nseq=2, nmacro=2, debug=(), stop=3):
    STOP = stop
    nc = bass.Bass("TRN2", target_bir_lowering=False)
    S = Sched()
    dbg_specs = {}

    def din(name, shape):
        return nc.dram_tensor(name, list(shape), F32, kind="ExternalInput").ap()

    xT = din("xT", [nseq, D, SEQ])
    metaT = din("metaT", [D, NMETA])
    consts_d = din("consts", [128, NCONST])
    rope_d = din("rope", [64, 2, LTOT])
    w_g = [din("ffn1_w_gate", [D, DFF]), din("ffn2_w_gate", [D, DFF])]
    w_u = [din("ffn1_w_up", [D, DFF]), din("ffn2_w_up", [D, DFF])]
    w_d = [din("ffn1_w_down", [DFF, D]), din("ffn2_w_down", [DFF, D])]
    w_in = din("w_in", [D, 1728])
    w_uq = din("w_uq", [384, 768])
    w_uk = din("w_uk", [256, 512])
    w_uv = din("w_uv", [256, 512])
    gate_a = din("gate_a_w", [8, 64, 64])
    gate_x = din("gate_x_w", [8, 64, 64])
    w_out = din("w_out", [D, D])
    outT = nc.dram_tensor("outT", [nseq, D, SEQ], F32, kind="ExternalOutput").ap()
    dbg_out = {}
    for name, shape in debug:
        dbg_out[name] = nc.dram_tensor("dbg_" + name, list(shape), F32, kind="ExternalOutput").ap()

    es = ExitStack()
    with es:
        def sb(name, shape, dt):
            return es.enter_context(nc.sbuf_tensor(name, list(shape), dt))

        h_t = sb("h", [128, 8, MACW], F32)
        xn_t = sb("xn", [128, 8, MACW], BF16)
        pool_t = sb("pool", [128, NGRAN * GW], BF16)
        aTm_t = sb("aTm", [128, NFH, NMETA], BF16)
        wring_t = sb("wring", [128, 6, 8, 128], BF16)
        wdring_t = sb("wdring", [128, 3, NFH, 128], BF16)
        kn_t = sb("kn", [128, NH, LTOT], BF16)
        kr_t = sb("kr", [64, NH, LTOT], BF16)
        v_t = sb("v", [128, NKT, 512], BF16)
        wuq_t = sb("wuq", [128, 3, 768], BF16)
        wuqrot_t = sb("wuqrot", [128, 3, 256], BF16)
        wuk_t = sb("wuk", [128, 2, 512], BF16)
        wuv_t = sb("wuv", [128, 2, 512], BF16)
        gat_t = sb("gat", [128, 2, 4, 128], BF16)
        cst_t = sb("cst", [128, NCONST], F32)
        der_t = sb("der", [128, 16], F32)
        ones_t = sb("ones", [128, 128], BF16)
        uhist_t = sb("uhist", [128, 4, 3], F32)
        hst_t = sb("hst", [128, 4], F32)
        uhsv_t = sb("uhsv", [128, 4, 3], F32)
        hssv_t = sb("hssv", [128, 4], F32)
        banks = [es.enter_context(nc.psum_tensor(f"ps{i}", [128, 512], F32)) for i in range(8)]

        pool = Pool(pool_t, NGRAN)
        B_h = [[Buf(f"h{c}_{s}") for s in range(3)] for c in range(8)]
        B_xn = [Buf(f"xn{s}") for s in range(3)]
        B_aTm = Buf("aTm")
        B_wr = [Buf(f"wr{i}") for i in range(6)]
        B_wd = [Buf(f"wd{i}") for i in range(3)]
        B_k = [Buf(f"k{i}") for i in range(NKT)]
        B_v = [Buf(f"v{i}") for i in range(NKT)]
        B_wq, B_wqr, B_wk, B_wv, B_gat = Buf("wq"), Buf("wqr"), Buf("wk"), Buf("wv"), Buf("gat")
        B_cst, B_der, B_ones, B_uh, B_hst = Buf("cst"), Buf("der"), Buf("ones"), Buf("uh"), Buf("hst")
        B_sv = Buf("sv")
        B_ps = [Buf(f"ps{i}", x=True) for i in range(8)]

        st = {"gen": 0, "acc": 0, "wr": 0, "wd": 0, "genset": [0, 1, 2, 3, 4, 5], "accset": [6, 7]}

        def ps_gen():
            gs = st["genset"]
            i = gs[st["gen"] % len(gs)]
            st["gen"] += 1
            return banks[i], B_ps[i]

        def ps_acc():
            a = st["accset"]
            i = a[st["acc"] % len(a)]
            st["acc"] += 1
            return banks[i], B_ps[i]

        def wr_next():
            i = st["wr"] % 6
            st["wr"] += 1
            return wring_t[:, i], B_wr[i]

        def wd_next():
            i = st["wd"] % 3
            st["wd"] += 1
            return wdring_t[:, i], B_wd[i]

        def cc(col, n=1, p=128):
            return cst_t[0:p, col:col + n]

        def tap(name, ap, reads, idx=None):
            if name in dbg_out:
                dst = dbg_out[name] if idx is None else dbg_out[name][idx]
                S.out_tokens.append(S.dma("sp", dst, ap, reads=reads))

        S.dma("sp", cst_t[:], consts_d, writes=[B_cst])
        S.op("dve", lambda e: e.memset(ones_t[:], 1.0), writes=[B_ones])
        S.op("dve", lambda e: e.memset(gat_t[:], 0.0), writes=[B_gat])
        SK = os.environ.get("K_SKIP", "")
        if "q" not in SK:
            S.dma("pool", wuq_t[:], w_uq.rearrange("(k p) n -> p k n", p=128), writes=[B_wq])
        S.dma("pool", wuk_t[:], w_uk.rearrange("(k p) n -> p k n", p=128), writes=[B_wk])
        S.dma("pool", wuv_t[:], w_uv.rearrange("(k p) n -> p k n", p=128), writes=[B_wv])
        for gi, gsrc in enumerate((gate_a, gate_x)):
            for n in range(8 if "g" not in SK else 0):
                o = 64 * (n % 2)
                S.dma("pool", gat_t[o:o + 64, gi, n // 2, o:o + 64], gsrc[n], writes=[B_gat])
        for hh in range(NH if "r" not in SK else 0):
            src0 = 192 * hh + 128
            S.op("dve", lambda e, hh=hh, src0=src0: e.tensor_scalar(
                wuqrot_t[:, :, 64 * hh:64 * hh + 32], wuq_t[:, :, src0 + 32:src0 + 64], -1.0, 0.0, ALU.mult, ALU.add),
                reads=[B_wq], writes=[B_wqr])
            S.op("dve", lambda e, hh=hh, src0=src0: e.tensor_copy(
                wuqrot_t[:, :, 64 * hh + 32:64 * hh + 64], wuq_t[:, :, src0:src0 + 32]),
                reads=[B_wq], writes=[B_wqr])
        S.op("dve", lambda e: e.tensor_scalar(der_t[:, 0:4], cc(C_BA, 4), 0.5, 0.0, ALU.mult, ALU.add),
             reads=[B_cst], writes=[B_der])
        S.op("dve", lambda e: e.tensor_scalar(der_t[:, 4:8], cc(C_BX, 4), 0.5, 0.0, ALU.mult, ALU.add),
             reads=[B_cst], writes=[B_der])
        S.op("act", lambda e: e.activation(out=der_t[:, 8:12], in_=cc(C_LAM, 4), func=AF.Exp, scale=-1.0),
             reads=[B_cst], writes=[B_der])
        S.op("act", lambda e: e.activation(out=der_t[:, 8:12], in_=der_t[:, 8:12], func=AF.Ln, bias=1.0, scale=1.0),
             reads=[B_der], writes=[B_der])
        S.op("dve", lambda e: e.tensor_scalar(der_t[:, 12:16], der_t[:, 8:12], -8.0, 0.0, ALU.mult, ALU.add),
             reads=[B_der], writes=[B_der])
        S.op("dve", lambda e: e.tensor_scalar(der_t[:, 8:12], der_t[:, 8:12], -4.0, 0.0, ALU.mult, ALU.add),
             reads=[B_der], writes=[B_der])

        def rstd_from(ssq_ps, ssq_b, pw, n, eps):
            r = pool.f()
            S.op("act", lambda e: e.activation(out=r.ap[:, 0:pw], in_=ssq_ps[:, 0:pw], func=AF.Ln,
                                               bias=float(eps), scale=1.0 / n),
                 reads=[ssq_b], writes=r.bufs)
            S.op("act", lambda e: e.activation(out=r.ap[:, 0:pw], in_=r.ap[:, 0:pw], func=AF.Exp, scale=-0.5),
                 reads=r.bufs, writes=r.bufs)
            return r

        def sq_accum(ssq_ps, ssq_b, sq_slot, p, pw, first, last):
            S.op("pe", lambda e: e.matmul(ssq_ps[:, 0:pw], ones_t[0:p, :], sq_slot.ap[0:p, 0:pw],
                                          start=first, stop=last),
                 reads=[B_ones] + sq_slot.bufs, writes=[ssq_b])

        class NormAcc:
            def __init__(self, si, o, w, bank):
                self.si, self.o, self.w = si, o, w
                self.ps, self.pb = banks[bank], B_ps[bank]
                self.n = 0
                self.pend = None

            def _flush(self, last):
                if self.pend is not None:
                    c, sq = self.pend
                    sq_accum(self.ps, self.pb, sq, 128, self.w, c == 0, last)
                    pool.rel(sq)
                    self.pend = None

            def add(self, c):
                o, w, si = self.o, self.w, self.si
                self._flush(False)
                sq = pool.b()
                S.op("act", lambda e: e.activation(out=sq.ap[:, 0:w], in_=h_t[:, c, o:o + w], func=AF.Square),
                     reads=[B_h[c][si]], writes=sq.bufs)
                self.pend = (self.n, sq)
                self.n += 1

            def rstd(self):
                assert self.n == 8
                self._flush(True)
                return rstd_from(self.ps, self.pb, self.w, D, EPS)

            def to_xn(self, gcol):
                o, w, si = self.o, self.w, self.si
                r = self.rstd()
                for c in range(8):
                    S.op("dve", lambda e, c=c: e.scalar_tensor_tensor(
                        xn_t[:, c, o:o + w], h_t[:, c, o:o + w], cc(gcol + c), r.ap[:, 0:w], ALU.mult, ALU.mult),
                        reads=[B_h[c][si], B_cst] + r.bufs, writes=[B_xn[si]])
                pool.rel(r)

            def to_out(self, seq, F0):
                o, w, si = self.o, self.w, self.si
                r = self.rstd()
                obs = [pool.f() for _ in range(4)]
                for c in range(8):
                    ob = obs[c % 4]
                    S.op("dve", lambda e, c=c, ob=ob: e.scalar_tensor_tensor(
                        ob.ap[:, 0:w], h_t[:, c, o:o + w], cc(C_FIN + c), r.ap[:, 0:w], ALU.mult, ALU.mult),
                        reads=[B_h[c][si], B_cst] + r.bufs, writes=ob.bufs)
                    S.out_tokens.append(S.dma("sp", outT[seq, c * 128:(c + 1) * 128, F0:F0 + w], ob.ap[:, 0:w], reads=ob.bufs))
                pool.rel(r, *obs)

        def norm_h(subs, gcol):
            for (si, o, w) in subs:
                ssq_ps, ssq_b = ps_acc()
                sqs = []
                for c in range(8):
                    sq = pool.b()
                    S.op("act", lambda e, c=c, sq=sq: e.activation(out=sq.ap[:, 0:w], in_=h_t[:, c, o:o + w], func=AF.Square),
                         reads=[B_h[c][si]], writes=sq.bufs)
                    sqs.append(sq)
                for c in range(8):
                    sq_accum(ssq_ps, ssq_b, sqs[c], 128, w, c == 0, c == 7)
                    pool.rel(sqs[c])
                r = rstd_from(ssq_ps, ssq_b, w, D, EPS)
                for c in range(8):
                    S.op("dve", lambda e, c=c: e.scalar_tensor_tensor(
                        xn_t[:, c, o:o + w], h_t[:, c, o:o + w], cc(gcol + c), r.ap[:, 0:w], ALU.mult, ALU.mult),
                        reads=[B_h[c][si], B_cst] + r.bufs, writes=[B_xn[si]])
                pool.rel(r)

        def ffn(k, subs, gcol, hooks=None):
            if gcol is not None:
                norm_h(subs, gcol)
            st["genset"] = [0, 1, 2, 3, 4]
            hooks = hooks or {}
            if "start" in hooks:
                hooks["start"]()
            accs = {si: NormAcc(si, o, w, 5 + si) for (si, o, w) in subs}
            for half in range(2):
                aT = {}
                for j in range(NFH):
                    f = half * NFH + j
                    wg_ap, wg_b = wr_next()
                    wu_ap, wu_b = wr_next()
                    S.dma("pool", wg_ap, w_g[k].rearrange("(c p) n -> p c n", p=128)[:, :, f * 128:(f + 1) * 128],
                          writes=[wg_b])
                    S.dma("pool", wu_ap, w_u[k].rearrange("(c p) n -> p c n", p=128)[:, :, f * 128:(f + 1) * 128],
                          writes=[wu_b])
                    for (si, o, w) in subs:
                        pg, pg_b = ps_gen()
                        pu, pu_b = ps_gen()

                        def mm(e, wt=wg_ap, ps=pg, o=o, w=w):
                            for c in range(8):
                                ins = e.matmul(ps[:, 0:w], wt[:, c, :], xn_t[:, c, o:o + w], start=(c == 0), stop=(c == 7))
                            return ins
                        S.op("pe", mm, reads=[wg_b, B_xn[si]], writes=[pg_b])
                        S.op("pe", lambda e, wt=wu_ap, ps=pu, o=o, w=w: mm(e, wt, ps, o, w),
                             reads=[wu_b, B_xn[si]], writes=[pu_b])
                        sg = pool.f()
                        S.op("act", lambda e, sg=sg, pg=pg, w=w: e.activation(out=sg.ap[:, 0:w], in_=pg[:, 0:w], func=AF.Silu),
                             reads=[pg_b], writes=sg.bufs)
                        if si == 0:
                            dst, dst_b = aTm_t[:, j, 0:w], [B_aTm]
                        else:
                            a = pool.b()
                            aT[(j, si)] = a
                            dst, dst_b = a.ap[:, 0:w], a.bufs
                        S.op("dve", lambda e, dst=dst, sg=sg, pu=pu, w=w: e.tensor_tensor(dst, sg.ap[:, 0:w], pu[:, 0:w], ALU.mult),
                             reads=sg.bufs + [pu_b], writes=dst_b)
                        pool.rel(sg)
                if ("gu%d" % half) in hooks:
                    hooks["gu%d" % half]()
                for c in range(8):
                    wd_ap, wd_b = wd_next()
                    S.dma("pool", wd_ap,
                          w_d[k][half * NFH * 128:(half + 1) * NFH * 128, c * 128:(c + 1) * 128].rearrange("(j p) n -> p j n", p=128),
                          writes=[wd_b])
                    for (si, o, w) in subs:
                        pd, pd_b = ps_gen()

                        def mmd(e, si=si, w=w, pd=pd, wd_ap=wd_ap):
                            for j in range(NFH):
                                rhs = aTm_t[:, j, 0:w] if si == 0 else aT[(j, si)].ap[:, 0:w]
                                ins = e.matmul(pd[:, 0:w], wd_ap[:, j, :], rhs, start=(j == 0), stop=(j == NFH - 1))
                            return ins
                        rb = [B_aTm] if si == 0 else [b for j in range(NFH) for b in aT[(j, si)].bufs]
                        S.op("pe", mmd, reads=[wd_b] + rb, writes=[pd_b])
                        S.op("dve", lambda e, c=c, o=o, w=w, pd=pd: e.scalar_tensor_tensor(
                            h_t[:, c, o:o + w], pd[:, 0:w], 0.5, h_t[:, c, o:o + w], ALU.mult, ALU.add),
                            reads=[pd_b, B_h[c][si]], writes=[B_h[c][si]])
                        if half == 1:
                            accs[si].add(c)
                for a in aT.values():
                    pool.rel(a)
            st["genset"] = [0, 1, 2, 3, 4, 5]
            return accs

        CQ = [(0, 128), (128, 128), (256, 128)]
        CKV = [(384, 128), (512, 128)]
        KRc = (640, 64)
        Uc = [(704 + 128 * c, 128) for c in range(4)]
        Gc = [(1216 + 128 * c, 128) for c in range(4)]

        def mixer(seq, si, o, w, P0, is_meta):
            LV = float(os.environ.get('K_MIX', '9'))
            st['genset'], st['accset'] = [0, 1, 2], [3, 4]
            if LV <= 0:
                return
            kt0 = 0 if is_meta else 1 + (P0 - NMETA) // 128
            nkb = 1 if is_meta else w // 128
            winr = w_in.rearrange("(c p) n -> p c n", p=128)

            def win_group(col0, M, negrot=False):
                wt, wb = wr_next()
                if not negrot:
                    S.dma("pool", wt[:, :, 0:M], winr[:, :, col0:col0 + M], writes=[wb])
                else:
                    S.dma("pool", wt[:, :, 32:64], winr[:, :, col0:col0 + 32], writes=[wb])
                    S.dma("pool", wt[:, :, 0:32], winr[:, :, col0 + 32:col0 + 64], writes=[wb])
                    S.op("dve", lambda e: e.tensor_scalar(wt[:, :, 0:32], wt[:, :, 0:32], -1.0, 0.0, ALU.mult, ALU.add),
                         reads=[wb], writes=[wb])
                ps, pb = ps_gen()

                def mm(e):
                    for c in range(8):
                        ins = e.matmul(ps[0:M, 0:w], wt[:, c, 0:M], xn_t[:, c, o:o + w], start=(c == 0), stop=(c == 7))
                    return ins
                S.op("pe", mm, reads=[wb, B_xn[si]], writes=[pb])
                return ps, pb

            def raw_and_sq(ps, pb, p, need_raw=True):
                sq = pool.b()
                S.op("act", lambda e: e.activation(out=sq.ap[0:p, 0:w], in_=ps[0:p, 0:w], func=AF.Square),
                     reads=[pb], writes=sq.bufs)
                raw = None
                if need_raw:
                    raw = pool.f()
                    S.op("dve", lambda e: e.tensor_copy(raw.ap[0:p, 0:w], ps[0:p, 0:w]), reads=[pb], writes=raw.bufs)
                return raw, sq

            def latent(blocks, gcol, n):
                ssq_ps, ssq_b = ps_acc()
                raws, pend = [], None
                for i, (c0, M) in enumerate(blocks):
                    ps, pb = win_group(c0, M)
                    raw, sq = raw_and_sq(ps, pb, 128)
                    raws.append(raw)
                    if pend is not None:
                        sq_accum(ssq_ps, ssq_b, pend[1], 128, w, pend[0] == 0, False)
                        pool.rel(pend[1])
                    pend = (i, sq)
                sq_accum(ssq_ps, ssq_b, pend[1], 128, w, pend[0] == 0, True)
                pool.rel(pend[1])
                r = rstd_from(ssq_ps, ssq_b, w, n, EPS)
                outs = []
                for i, raw in enumerate(raws):
                    ob = pool.b()
                    S.op("dve", lambda e, i=i, raw=raw, ob=ob: e.scalar_tensor_tensor(
                        ob.ap[:, 0:w], raw.ap[:, 0:w], cc(gcol + i), r.ap[:, 0:w], ALU.mult, ALU.mult),
                        reads=raw.bufs + r.bufs + [B_cst], writes=ob.bufs)
                    pool.rel(raw)
                    outs.append(ob)
                pool.rel(r)
                return outs

            cosS, sinS = pool.f(), pool.f()
            S.dma("sp", cosS.ap[0:64, 0:w], rope_d[:, 0, P0:P0 + w], writes=cosS.bufs)
            S.dma("sp", sinS.ap[0:64, 0:w], rope_d[:, 1, P0:P0 + w], writes=sinS.bufs)

            def roped(ps_r, pb_r, ps_rot, pb_rot, gcol_r, gcol_rp):
                t1, t2 = pool.f(), pool.f()
                S.op("dve", lambda e: e.scalar_tensor_tensor(t1.ap[0:64, 0:w], ps_r[0:64, 0:w], cc(gcol_r, 1, 64),
                                                             cosS.ap[0:64, 0:w], ALU.mult, ALU.mult),
                     reads=[pb_r, B_cst] + cosS.bufs, writes=t1.bufs)
                S.op("dve", lambda e: e.scalar_tensor_tensor(t2.ap[0:64, 0:w], ps_rot[0:64, 0:w], cc(gcol_rp, 1, 64),
                                                             sinS.ap[0:64, 0:w], ALU.mult, ALU.mult),
                     reads=[pb_rot, B_cst] + sinS.bufs, writes=t2.bufs)
                S.op("dve", lambda e: e.tensor_tensor(t1.ap[0:64, 0:w], t1.ap[0:64, 0:w], t2.ap[0:64, 0:w], ALU.add),
                     reads=t1.bufs + t2.bufs, writes=t1.bufs)
                pool.rel(t2)
                return t1

            if LV <= 0.5:
                pool.free = [True] * pool.n
                return
            cqn = None
            if not is_meta:
                cqn = latent(CQ, C_QLAT, 384)
            ckvn = latent(CKV, C_KVLAT, 256)

            if LV <= 1:
                pool.free = [True] * pool.n
                return
            ps_kr, pb_kr = win_group(KRc[0], 64)
            ps_krot, pb_krot = win_group(KRc[0], 64, negrot=True)
            _, sq_kr = raw_and_sq(ps_kr, pb_kr, 64, need_raw=False)
            kroped = roped(ps_kr, pb_kr, ps_krot, pb_krot, C_KR, C_KRP)

            if LV <= 2:
                pool.free = [True] * pool.n
                return
            ub, graw = [], []
            for c in range(4):
                ps, pb = win_group(*Uc[c])
                u = pool.f()
                S.op("act", lambda e, u=u, ps=ps: e.activation(out=u.ap[:, 3:3 + w], in_=ps[:, 0:w], func=AF.Copy),
                     reads=[pb], writes=u.bufs)
                S.op("dve", lambda e, u=u, c=c: e.tensor_copy(u.ap[:, 0:3], uhist_t[:, c, :]), reads=[B_uh], writes=u.bufs)
                ub.append(u)
                if not is_meta:
                    ps, pb = win_group(*Gc[c])
                    g = pool.f()
                    S.op("act", lambda e, g=g, ps=ps: e.activation(out=g.ap[:, 0:w], in_=ps[:, 0:w], func=AF.Copy),
                         reads=[pb], writes=g.bufs)
                    graw.append(g)
            for c in range(4):
                S.op("dve", lambda e, c=c: e.tensor_copy(uhist_t[:, c, :], ub[c].ap[:, w:w + 3]),
                     reads=ub[c].bufs, writes=[B_uh])

            yln = []
            lb = {"i": 0}

            def lru_bank():
                i = 5 + lb["i"] % 2
                lb["i"] += 1
                return banks[i], B_ps[i]

            def YOP(*a, **k):
                return S.op(*a, **k)

            def lru_gen():
                yl = []
                for c in range(4):
                    u = ub[c]
                    xc = pool.f()
                    yield YOP("dve", lambda e, xc=xc, u=u, c=c: e.tensor_scalar(xc.ap[:, 0:w], u.ap[:, 0:w], cc(C_CW + c), cc(C_CB + c), ALU.mult, ALU.add),
                         reads=u.bufs + [B_cst], writes=xc.bufs)
                    for j in range(1, 4):
                        yield YOP("dve", lambda e, xc=xc, u=u, c=c, j=j: e.scalar_tensor_tensor(
                            xc.ap[:, 0:w], u.ap[:, j:j + w], cc(C_CW + 4 * j + c), xc.ap[:, 0:w], ALU.mult, ALU.add),
                            reads=u.bufs + xc.bufs + [B_cst], writes=xc.bufs)
                    pool.rel(u)
                    xcb = pool.b()
                    yield YOP("dve", lambda e, xcb=xcb, xc=xc: e.tensor_copy(xcb.ap[:, 0:w], xc.ap[:, 0:w]), reads=xc.bufs, writes=xcb.bufs)
                    pa, pa_b = lru_bank()
                    px, px_b = lru_bank()
                    yield YOP("pe", lambda e, pa=pa, c=c, xcb=xcb: e.matmul(pa[:, 0:w], gat_t[:, 0, c, :], xcb.ap[:, 0:w], start=True, stop=True),
                         reads=[B_gat] + xcb.bufs, writes=[pa_b])
                    yield YOP("pe", lambda e, px=px, c=c, xcb=xcb: e.matmul(px[:, 0:w], gat_t[:, 1, c, :], xcb.ap[:, 0:w], start=True, stop=True),
                         reads=[B_gat] + xcb.bufs, writes=[px_b])
                    pool.rel(xcb)
                    tr, ti = pool.f(), pool.f()
                    yield YOP("act", lambda e, tr=tr, pa=pa, c=c: e.activation(out=tr.ap[:, 0:w], in_=pa[:, 0:w], func=AF.Tanh, bias=der_t[:, c:c + 1], scale=0.5),
                         reads=[pa_b, B_der], writes=tr.bufs)
                    yield YOP("act", lambda e, ti=ti, px=px, c=c: e.activation(out=ti.ap[:, 0:w], in_=px[:, 0:w], func=AF.Tanh, bias=der_t[:, 4 + c:5 + c], scale=0.5),
                         reads=[px_b, B_der], writes=ti.bufs)
                    a1, a2 = pool.f(), pool.f()
                    yield YOP("act", lambda e, a1=a1, tr=tr, c=c: e.activation(out=a1.ap[:, 0:w], in_=tr.ap[:, 0:w], func=AF.Exp,
                                                                           bias=der_t[:, 8 + c:9 + c], scale=der_t[:, 8 + c:9 + c]),
                         reads=tr.bufs + [B_der], writes=a1.bufs)
                    yield YOP("act", lambda e, a2=a2, tr=tr, c=c: e.activation(out=a2.ap[:, 0:w], in_=tr.ap[:, 0:w], func=AF.Exp,
                                                                           bias=der_t[:, 12 + c:13 + c], scale=der_t[:, 12 + c:13 + c]),
                         reads=tr.bufs + [B_der], writes=a2.bufs)
                    pool.rel(tr)
                    yield YOP("act", lambda e, a2=a2: e.activation(out=a2.ap[:, 0:w], in_=a2.ap[:, 0:w], func=AF.Ln, bias=1.0, scale=-1.0),
                         reads=a2.bufs, writes=a2.bufs)
                    yield YOP("act", lambda e, a2=a2: e.activation(out=a2.ap[:, 0:w], in_=a2.ap[:, 0:w], func=AF.Exp, scale=0.5),
                         reads=a2.bufs, writes=a2.bufs)
                    yield YOP("dve", lambda e, ti=ti, xc=xc: e.scalar_tensor_tensor(ti.ap[:, 0:w], ti.ap[:, 0:w], 1.0, xc.ap[:, 0:w], ALU.add, ALU.mult),
                         reads=ti.bufs + xc.bufs, writes=ti.bufs)
                    pool.rel(xc)
                    yield YOP("dve", lambda e, a2=a2, ti=ti: e.scalar_tensor_tensor(a2.ap[:, 0:w], a2.ap[:, 0:w], 0.5, ti.ap[:, 0:w], ALU.mult, ALU.mult),
                         reads=a2.bufs + ti.bufs, writes=a2.bufs)
                    if is_meta:
                        yield YOP("dve", lambda e, a2=a2, ti=ti: e.tensor_scalar(a2.ap[:, 0:1], ti.ap[:, 0:1], 0.5, 0.0, ALU.mult, ALU.add),
                             reads=a2.bufs + ti.bufs, writes=a2.bufs)
                    hl = ti
                    init = 0.0 if is_meta else hst_t[:, c:c + 1]
                    yield YOP("dve", lambda e, hl=hl, a1=a1, a2=a2, init=init: e.tensor_tensor_scan(hl.ap[:, 0:w], a1.ap[:, 0:w], a2.ap[:, 0:w], init, ALU.mult, ALU.add),
                         reads=a1.bufs + a2.bufs + [B_hst], writes=hl.bufs)
                    yield YOP("dve", lambda e, hl=hl, c=c: e.tensor_copy(hst_t[:, c:c + 1], hl.ap[:, w - 1:w]), reads=hl.bufs, writes=[B_hst])
                    pool.rel(a1, a2)
                    if is_meta:
                        pool.rel(hl)
                        continue
                    g = graw[c]
                    g2 = pool.f()
                    yield YOP("act", lambda e, g2=g2, g=g: e.activation(out=g2.ap[:, 0:w], in_=g.ap[:, 0:w], func=AF.Square),
                         reads=g.bufs, writes=g2.bufs)
                    yield YOP("dve", lambda e, g2=g2: e.tensor_scalar(g2.ap[:, 0:w], g2.ap[:, 0:w], 0.044715, 1.0, ALU.mult, ALU.add),
                         reads=g2.bufs, writes=g2.bufs)
                    yield YOP("dve", lambda e, g2=g2, g=g: e.tensor_tensor(g2.ap[:, 0:w], g2.ap[:, 0:w], g.ap[:, 0:w], ALU.mult),
                         reads=g2.bufs + g.bufs, writes=g2.bufs)
                    yield YOP("act", lambda e, g2=g2: e.activation(out=g2.ap[:, 0:w], in_=g2.ap[:, 0:w], func=AF.Tanh, scale=0.7978845608028654),
                         reads=g2.bufs, writes=g2.bufs)
                    yield YOP("dve", lambda e, g2=g2, g=g: e.scalar_tensor_tensor(g2.ap[:, 0:w], g2.ap[:, 0:w], 1.0, g.ap[:, 0:w], ALU.add, ALU.mult),
                         reads=g2.bufs + g.bufs, writes=g2.bufs)
                    yield YOP("dve", lambda e, g2=g2, hl=hl: e.scalar_tensor_tensor(hl.ap[:, 0:w], g2.ap[:, 0:w], 0.5, hl.ap[:, 0:w], ALU.mult, ALU.mult),
                         reads=g2.bufs + hl.bufs, writes=hl.bufs)
                    pool.rel(g2, g)
                    yl.append(hl)
                if is_meta:
                    return
                yield
                tap("ylru", yl[0].ap[:, 0:w], yl[0].bufs)
                ssq_ps, ssq_b = banks[7], B_ps[7]
                for c in range(4):
                    sq = pool.b()
                    yield YOP("act", lambda e, sq=sq, c=c: e.activation(out=sq.ap[:, 0:w], in_=yl[c].ap[:, 0:w], func=AF.Square),
                         reads=yl[c].bufs, writes=sq.bufs)
                    yield sq_accum(ssq_ps, ssq_b, sq, 128, w, c == 0, c == 3)
                    pool.rel(sq)
                r = rstd_from(ssq_ps, ssq_b, w, 512, EPS)
                yield
                for c in range(4):
                    ob = pool.b()
                    yield YOP("dve", lambda e, ob=ob, c=c, r=r: e.scalar_tensor_tensor(
                        ob.ap[:, 0:w], yl[c].ap[:, 0:w], cc(C_LO + c), r.ap[:, 0:w], ALU.mult, ALU.mult),
                        reads=yl[c].bufs + r.bufs + [B_cst], writes=ob.bufs)
                    yln.append(ob)
                pool.rel(r, *yl)


            lru_it = lru_gen()

            def lru_step(n):
                for _ in range(n):
                    try:
                        next(lru_it)
                    except StopIteration:
                        return

            if LV <= 3:
                pool.free = [True] * pool.n
                return
            kbufs = [B_k[kt0 + i] for i in range(nkb)]
            st['genset'] = [0, 1, 2, 3, 4]
            kps, ksq, knrs = [], [], []
            for hh in range(NH):
                ps, pb = ps_gen()

                def mmk(e, ps=ps, hh=hh):
                    for kc in range(2):
                        ins = e.matmul(ps[:, 0:w], wuk_t[:, kc, hh * 128:(hh + 1) * 128], ckvn[kc].ap[:, 0:w],
                                       start=(kc == 0), stop=(kc == 1))
                    return ins
                S.op("pe", mmk, reads=[B_wk] + ckvn[0].bufs + ckvn[1].bufs, writes=[pb])
                kps.append((ps, pb))
            for hh in range(NH):
                ps, pb = kps[hh]
                sq = pool.b()
                S.op("act", lambda e, sq=sq, ps=ps: e.activation(out=sq.ap[:, 0:w], in_=ps[:, 0:w], func=AF.Square),
                     reads=[pb], writes=sq.bufs)
                knr = pool.f()
                S.op("dve", lambda e, knr=knr, ps=ps: e.tensor_scalar(knr.ap[:, 0:w], ps[:, 0:w], cc(C_KN), None, ALU.mult),
                     reads=[pb, B_cst], writes=knr.bufs)
                ksq.append(sq)
                knrs.append(knr)
            lru_step(6)
            kss = []
            for hh in range(NH):
                ps, pb = ps_gen()

                def mmss(e, ps=ps, sq=ksq[hh]):
                    e.matmul(ps[:, 0:w], ones_t[:, :], sq.ap[:, 0:w], start=True, stop=False)
                    return e.matmul(ps[:, 0:w], ones_t[0:64, :], sq_kr.ap[0:64, 0:w], start=False, stop=True)
                S.op("pe", mmss, reads=[B_ones] + ksq[hh].bufs + sq_kr.bufs, writes=[pb])
                kss.append((ps, pb))
            krs = []
            for hh in range(NH):
                ps, pb = kss[hh]
                r = pool.f()
                S.op("act", lambda e, r=r, ps=ps: e.activation(out=r.ap[:, 0:w], in_=ps[:, 0:w], func=AF.Ln, bias=float(EPS), scale=1.0 / 192),
                     reads=[pb], writes=r.bufs)
                krs.append(r)
            for hh in range(NH):
                r = krs[hh]
                S.op("act", lambda e, r=r: e.activation(out=r.ap[:, 0:w], in_=r.ap[:, 0:w], func=AF.Exp, scale=-0.5),
                     reads=r.bufs, writes=r.bufs)
            lru_step(6)
            for hh in range(NH):
                r, knr = krs[hh], knrs[hh]
                S.op("dve", lambda e, hh=hh, knr=knr, r=r: e.tensor_tensor(kn_t[:, hh, P0:P0 + w], knr.ap[:, 0:w], r.ap[:, 0:w], ALU.mult),
                     reads=knr.bufs + r.bufs, writes=kbufs)
                S.op("dve", lambda e, hh=hh, r=r: e.tensor_tensor(kr_t[:, hh, P0:P0 + w], kroped.ap[0:64, 0:w], r.ap[0:64, 0:w], ALU.mult),
                     reads=kroped.bufs + r.bufs, writes=kbufs)
                pool.rel(knr, r, ksq[hh])
            pool.rel(sq_kr, kroped)
            for tb in range(nkb):
                nt = NMETA if is_meta else 128
                ps, pb = ps_gen()

                def mmv(e, ps=ps, tb=tb, nt=nt):
                    for kc in range(2):
                        ins = e.matmul(ps[0:nt, :], ckvn[kc].ap[:, tb * 128:tb * 128 + nt], wuv_t[:, kc, :],
                                       start=(kc == 0), stop=(kc == 1))
                    return ins
                S.op("pe", mmv, reads=[B_wv] + ckvn[0].bufs + ckvn[1].bufs, writes=[pb])
                S.op("act", lambda e, ps=ps, tb=tb, nt=nt: e.activation(out=v_t[0:nt, kt0 + tb, :], in_=ps[0:nt, :], func=AF.Copy),
                     reads=[pb], writes=[B_v[kt0 + tb]])
            pool.rel(*ckvn)
            st['genset'] = [0, 1, 2]

            if LV <= 4:
                pool.free = [True] * pool.n
                return
            ymn = []
            if not is_meta:
                Qn, Qr = [None] * NH, [None] * NH
                rq = [b for s_ in cqn for b in s_.bufs]
                for batch in ((0, 1), (2, 3)):
                    held = {}
                    for hh in batch:
                        psn, pbn = ps_gen()
                        psr, pbr = ps_gen()
                        pso, pbo = ps_gen()

                        def mmq(e, ps, M, lhs):
                            for kc in range(3):
                                ins = e.matmul(ps[0:M, 0:w], lhs(kc), cqn[kc].ap[:, 0:w], start=(kc == 0), stop=(kc == 2))
                            return ins
                        S.op("pe", lambda e, hh=hh, psn=psn: mmq(e, psn, 128, lambda kc: wuq_t[:, kc, 192 * hh:192 * hh + 128]),
                             reads=[B_wq] + rq, writes=[pbn])
                        S.op("pe", lambda e, hh=hh, psr=psr: mmq(e, psr, 64, lambda kc: wuq_t[:, kc, 192 * hh + 128:192 * hh + 192]),
                             reads=[B_wq] + rq, writes=[pbr])
                        S.op("pe", lambda e, hh=hh, pso=pso: mmq(e, pso, 64, lambda kc: wuqrot_t[:, kc, 64 * hh:64 * hh + 64]),
                             reads=[B_wqr] + rq, writes=[pbo])
                        sqn, sqr = pool.b(), pool.b()
                        S.op("act", lambda e, sqn=sqn, psn=psn: e.activation(out=sqn.ap[:, 0:w], in_=psn[:, 0:w], func=AF.Square),
                             reads=[pbn], writes=sqn.bufs)
                        S.op("act", lambda e, sqr=sqr, psr=psr: e.activation(out=sqr.ap[0:64, 0:w], in_=psr[0:64, 0:w], func=AF.Square),
                             reads=[pbr], writes=sqr.bufs)
                        qnr = pool.f()
                        S.op("dve", lambda e, qnr=qnr, psn=psn: e.tensor_scalar(qnr.ap[:, 0:w], psn[:, 0:w], cc(C_QN), None, ALU.mult),
                             reads=[pbn, B_cst], writes=qnr.bufs)
                        qrp = roped(psr, pbr, pso, pbo, C_QR, C_QRP)
                        held[hh] = (sqn, sqr, qnr, qrp)
                    lru_step(4)
                    sss = {}
                    for hh in batch:
                        sqn, sqr, qnr, qrp = held[hh]
                        ps, pb = ps_acc()

                        def mmss(e, ps=ps, sqn=sqn, sqr=sqr):
                            e.matmul(ps[:, 0:w], ones_t[:, :], sqn.ap[:, 0:w], start=True, stop=False)
                            return e.matmul(ps[:, 0:w], ones_t[0:64, :], sqr.ap[0:64, 0:w], start=False, stop=True)
                        S.op("pe", mmss, reads=[B_ones] + sqn.bufs + sqr.bufs, writes=[pb])
                        sss[hh] = (ps, pb)
                    rs = {}
                    for hh in batch:
                        ps, pb = sss[hh]
                        r = pool.f()
                        S.op("act", lambda e, r=r, ps=ps: e.activation(out=r.ap[:, 0:w], in_=ps[:, 0:w], func=AF.Ln, bias=float(EPS), scale=1.0 / 192),
                             reads=[pb], writes=r.bufs)
                        rs[hh] = r
                    for hh in batch:
                        r = rs[hh]
                        S.op("act", lambda e, r=r: e.activation(out=r.ap[:, 0:w], in_=r.ap[:, 0:w], func=AF.Exp, scale=-0.5),
                             reads=r.bufs, writes=r.bufs)
                    for hh in batch:
                        sqn, sqr, qnr, qrp = held[hh]
                        r = rs[hh]
                        qn_b, qr_b = pool.b(), pool.b()
                        S.op("dve", lambda e, qn_b=qn_b, qnr=qnr, r=r: e.tensor_tensor(qn_b.ap[:, 0:w], qnr.ap[:, 0:w], r.ap[:, 0:w], ALU.mult),
                             reads=qnr.bufs + r.bufs, writes=qn_b.bufs)
                        S.op("dve", lambda e, qr_b=qr_b, qrp=qrp, r=r: e.tensor_tensor(qr_b.ap[0:64, 0:w], qrp.ap[0:64, 0:w], r.ap[0:64, 0:w], ALU.mult),
                             reads=qrp.bufs + r.bufs, writes=qr_b.bufs)
                        pool.rel(sqn, sqr, qnr, qrp, r)
                        Qn[hh], Qr[hh] = qn_b, qr_b
                    lru_step(4)
                pool.rel(*cqn)
                pool.rel(cosS, sinS)

                if LV <= 5:
                    pool.free = [True] * pool.n
                    return
                nqt = w // 128
                kt_last = kt0 + nqt - 1
                sc = 1.0 / float(np.sqrt(192.0))
                ymla = []
                ssq_mla, ssq_mla_b = None, None
                for hh in range(NH):
                    po, po_b = banks[3], B_ps[3]
                    prs, prs_b = banks[4], B_ps[4]

                    def kinfo(kt):
                        if kt == 0:
                            return 0, NMETA, 0
                        lq = max(0, kt - kt0)
                        return NMETA + (kt - 1) * 128, 128, lq * 128

                    def emit_s(kt, hh=hh):
                        kp, nk, q0 = kinfo(kt)
                        i = st["s4"] % 3
                        st["s4"] += 1
                        ps, pb = banks[i], B_ps[i]

                        def mms(e):
                            e.matmul(ps[0:nk, q0:w], kn_t[:, hh, kp:kp + nk], Qn[hh].ap[:, q0:w], start=True, stop=False)
                            return e.matmul(ps[0:nk, q0:w], kr_t[0:64, hh, kp:kp + nk], Qr[hh].ap[0:64, q0:w], start=False, stop=True)
                        S.op("pe", mms, reads=[B_k[kt]] + Qn[hh].bufs + Qr[hh].bufs, writes=[pb])
                        pt = pool.b()
                        S.op("act", lambda e: e.activation(out=pt.ap[0:nk, q0:w], in_=ps[0:nk, q0:w], func=AF.Exp, scale=sc),
                             reads=[pb], writes=pt.bufs)
                        if kt >= kt0 and kt > 0:
                            S.op("dve", lambda e: e.memset(pt.ap[64:128, q0:q0 + 64], 0.0), writes=pt.bufs)
                        return pt

                    def emit_pv(kt, pt, hh=hh, po=po, po_b=po_b, prs=prs, prs_b=prs_b):
                        kp, nk, q0 = kinfo(kt)
                        S.op("pe", lambda e: e.matmul(po[:, q0:w], v_t[0:nk, kt, hh * 128:(hh + 1) * 128], pt.ap[0:nk, q0:w],
                                                      start=(kt == 0), stop=(kt == kt_last), skip_group_check=True),
                             reads=[B_v[kt]] + pt.bufs, writes=[po_b])
                        S.op("pe", lambda e: e.matmul(prs[:, q0:w], ones_t[0:nk, :], pt.ap[0:nk, q0:w],
                                                      start=(kt == 0), stop=(kt == kt_last), skip_group_check=True),
                             reads=[B_ones] + pt.bufs, writes=[prs_b])
                        pool.rel(pt)

                    st.setdefault("s4", 0)
                    pts = {0: emit_s(0)}
                    if kt_last >= 1:
                        pts[1] = emit_s(1)
                    for kt in range(0, kt_last + 1):
                        if kt + 2 <= kt_last:
                            pts[kt + 2] = emit_s(kt + 2)
                        emit_pv(kt, pts.pop(kt))
                        lru_step(2)
                    rc = pool.f()
                    S.op("dve", lambda e, rc=rc, prs=prs: e.reciprocal(rc.ap[:, 0:w], prs[:, 0:w]), reads=[prs_b], writes=rc.bufs)
                    y = pool.f()
                    S.op("dve", lambda e, y=y, rc=rc, po=po: e.tensor_tensor(y.ap[:, 0:w], po[:, 0:w], rc.ap[:, 0:w], ALU.mult),
                         reads=[po_b] + rc.bufs, writes=y.bufs)
                    pool.rel(rc)
                    ymla.append(y)
                for s_ in Qn + Qr:
                    pool.rel(s_)
                tap("ymla", ymla[0].ap[:, 0:w], ymla[0].bufs)
                ssq_ps, ssq_b = ps_acc()
                for hh in range(NH):
                    sq = pool.b()
                    S.op("act", lambda e, sq=sq, hh=hh: e.activation(out=sq.ap[:, 0:w], in_=ymla[hh].ap[:, 0:w], func=AF.Square),
                         reads=ymla[hh].bufs, writes=sq.bufs)
                    sq_accum(ssq_ps, ssq_b, sq, 128, w, hh == 0, hh == NH - 1)
                    pool.rel(sq)
                r = rstd_from(ssq_ps, ssq_b, w, 512, EPS)
                for hh in range(NH):
                    ob = pool.b()
                    S.op("dve", lambda e, ob=ob, hh=hh, r=r: e.scalar_tensor_tensor(
                        ob.ap[:, 0:w], ymla[hh].ap[:, 0:w], cc(C_AO + hh), r.ap[:, 0:w], ALU.mult, ALU.mult),
                        reads=ymla[hh].bufs + r.bufs + [B_cst], writes=ob.bufs)
                    ymn.append(ob)
                pool.rel(r, *ymla)
            else:
                pool.rel(cosS, sinS)

            if LV <= 6:
                pool.free = [True] * pool.n
                return
            lru_step(100000)
            if is_meta:
                st['genset'], st['accset'] = [0, 1, 2, 3, 4, 5], [6, 7]
                return
            ymn = ymn + yln
            acc2 = NormAcc(si, o, w, 5)
            woutr = w_out.rearrange("(c p) n -> p c n", p=128)
            for c in range(8):
                wt, wb = wr_next()
                S.dma("pool", wt, woutr[:, :, c * 128:(c + 1) * 128], writes=[wb])
                ps, pb = ps_gen()

                def mmo(e, ps=ps, wt=wt):
                    for kc in range(8):
                        ins = e.matmul(ps[:, 0:w], wt[:, kc, :], ymn[kc].ap[:, 0:w], start=(kc == 0), stop=(kc == 7))
                    return ins
                S.op("pe", mmo, reads=[wb] + [b for s_ in ymn for b in s_.bufs], writes=[pb])
                S.op("dve", lambda e, c=c, ps=ps: e.tensor_tensor(h_t[:, c, o:o + w], ps[:, 0:w], h_t[:, c, o:o + w], ALU.add),
                     reads=[pb, B_h[c][si]], writes=[B_h[c][si]])
                if STOP >= 3:
                    acc2.add(c)
            pool.rel(*ymn)
            if STOP >= 3:
                acc2.to_xn(C_FFN2)
            st['genset'], st['accset'] = [0, 1, 2, 3, 4, 5], [6, 7]

        def final_store(seq, si, o, w, F0):
            ssq_ps, ssq_b = ps_acc()
            sqs = []
            for c in range(8):
                sq = pool.b()
                S.op("act", lambda e, c=c, sq=sq: e.activation(out=sq.ap[:, 0:w], in_=h_t[:, c, o:o + w], func=AF.Square),
                     reads=[B_h[c][si]], writes=sq.bufs)
                sqs.append(sq)
            for c in range(8):
                sq_accum(ssq_ps, ssq_b, sqs[c], 128, w, c == 0, c == 7)
                pool.rel(sqs[c])
            r = rstd_from(ssq_ps, ssq_b, w, D, EPS)
            for c in range(8):
                ob = pool.f()
                S.op("dve", lambda e, c=c, ob=ob: e.scalar_tensor_tensor(
                    ob.ap[:, 0:w], h_t[:, c, o:o + w], cc(C_FIN + c), r.ap[:, 0:w], ALU.mult, ALU.mult),
                    reads=[B_h[c][si], B_cst] + r.bufs, writes=ob.bufs)
                S.out_tokens.append(S.dma("sp", outT[seq, c * 128:(c + 1) * 128, F0:F0 + w], ob.ap[:, 0:w], reads=ob.bufs))
                pool.rel(ob)
            pool.rel(r)

        def load_direct(seq, m, si):
            if si == 0:
                for c in range(8):
                    S.dma("sp", h_t[:, c, 0:NMETA], metaT[c * 128:(c + 1) * 128, :], writes=[B_h[c][0]])
            else:
                o = NMETA + (si - 1) * SUBW
                F0 = m * 2 * SUBW + (si - 1) * SUBW
                for c in range(8):
                    S.dma("sp", h_t[:, c, o:o + SUBW], xT[seq, c * 128:(c + 1) * 128, F0:F0 + SUBW], writes=[B_h[c][si]])

        class Prefetch:
            def __init__(self, seq, m):
                self.seq, self.m = seq, m
                self.stg = {}
                self.sq = {}

            def start(self):
                for si in (1, 2):
                    F0 = self.m * 2 * SUBW + (si - 1) * SUBW
                    self.stg[si] = [pool.f() for _ in range(8)]
                    for c in range(8):
                        sl = self.stg[si][c]
                        S.dma("sp", sl.ap[:, 0:SUBW], xT[self.seq, c * 128:(c + 1) * 128, F0:F0 + SUBW], writes=sl.bufs)

            def squares(self, sis=(1, 2)):
                for si in sis:
                    self.sq[si] = []
                    for c in range(8):
                        sl = self.stg[si][c]
                        sq = pool.b()
                        S.op("act", lambda e, sl=sl, sq=sq: e.activation(out=sq.ap[:, 0:SUBW], in_=sl.ap[:, 0:SUBW], func=AF.Square),
                             reads=sl.bufs, writes=sq.bufs)
                        self.sq[si].append(sq)

            def norms(self):
                if False:
                    sqs = []
                    for c in range(8):
                        sq = pool.b()
                        S.op("act", lambda e, c=c, sq=sq: e.activation(out=sq.ap[:, 0:NMETA], in_=h_t[:, c, 0:NMETA], func=AF.Square),
                             reads=[B_h[c][0]], writes=sq.bufs)
                        sqs.append(sq)
                    ps, pb = banks[5], B_ps[5]

                    def mm0(e, sqs=sqs, ps=ps):
                        for i, sq in enumerate(sqs):
                            ins = e.matmul(ps[:, 0:NMETA], ones_t[:, :], sq.ap[:, 0:NMETA], start=(i == 0), stop=(i == 7))
                        return ins
                    S.op("pe", mm0, reads=[B_ones] + [b for sq in sqs for b in sq.bufs], writes=[pb])
                    pool.rel(*sqs)
                    r = rstd_from(ps, pb, NMETA, D, EPS)
                    for c in range(8):
                        S.op("dve", lambda e, c=c, r=r: e.scalar_tensor_tensor(
                            xn_t[:, c, 0:NMETA], h_t[:, c, 0:NMETA], cc(C_FFN1 + c), r.ap[:, 0:NMETA], ALU.mult, ALU.mult),
                            reads=[B_h[c][0], B_cst] + r.bufs, writes=[B_xn[0]])
                    pool.rel(r)
                for si in (1, 2):
                    self.squares((si,))
                    o = NMETA + (si - 1) * SUBW
                    ps, pb = banks[5], B_ps[5]
                    sqs = self.sq[si]

                    def mm(e, sqs=sqs, ps=ps):
                        for i, sq in enumerate(sqs):
                            ins = e.matmul(ps[:, 0:SUBW], ones_t[:, :], sq.ap[:, 0:SUBW], start=(i == 0), stop=(i == 7))
                        return ins
                    S.op("pe", mm, reads=[B_ones] + [b for sq in sqs for b in sq.bufs], writes=[pb])
                    pool.rel(*sqs)
                    r = rstd_from(ps, pb, SUBW, D, EPS)
                    for c in range(8):
                        sl = self.stg[si][c]
                        S.op("dve", lambda e, c=c, sl=sl, o=o, r=r: e.scalar_tensor_tensor(
                            xn_t[:, c, o:o + SUBW], sl.ap[:, 0:SUBW], cc(C_FFN1 + c), r.ap[:, 0:SUBW], ALU.mult, ALU.mult),
                            reads=sl.bufs + [B_cst] + r.bufs, writes=[B_xn[si]])
                    pool.rel(r)

            def land(self):
                for si in (1, 2):
                    o = NMETA + (si - 1) * SUBW
                    for c in range(8):
                        sl = self.stg[si][c]
                        S.op("act", lambda e, c=c, sl=sl, o=o: e.activation(out=h_t[:, c, o:o + SUBW], in_=sl.ap[:, 0:SUBW], func=AF.Copy),
                             reads=sl.bufs, writes=[B_h[c][si]])
                        pool.rel(sl)

        macros = [(seq, m) for seq in range(nseq) for m in range(nmacro)]
        USE_PF = os.environ.get("K_PF", "1") == "1"
        prefetched = False
        for mi, (seq, m) in enumerate(macros):
            nxt = macros[mi + 1] if mi + 1 < len(macros) else None
            if m == 0 and seq == 0:
                S.op("dve", lambda e: e.memset(uhist_t[:], 0.0), writes=[B_uh])
                S.op("dve", lambda e: e.memset(hst_t[:], 0.0), writes=[B_hst])
            elif m == 0:
                S.op("dve", lambda e: e.tensor_copy(uhist_t[:], uhsv_t[:]), reads=[B_sv], writes=[B_uh])
                S.op("dve", lambda e: e.tensor_copy(hst_t[:], hssv_t[:]), reads=[B_sv], writes=[B_hst])
            subs = []
            if m == 0 and seq == 0:
                subs.append((0, 0, NMETA))
            subs += [(1, NMETA, SUBW), (2, NMETA + SUBW, SUBW)]
            fsubs = [s_ for s_ in subs if s_[0] != 0]
            if not prefetched:
                for (si, o, w) in subs:
                    load_direct(seq, m, si)
                accs = ffn(0, subs, C_FFN1)
            else:
                def deferred(prev_out=prev_out, prev_pf=prev_pf):
                    prev_out()
                    prev_pf.land()
                accs = ffn(0, subs, None, {"gu0": deferred})
            for (si, o, w) in subs:
                accs[si].to_xn(C_MIX)
            for (si, o, w) in subs:
                P0 = 0 if si == 0 else NMETA + m * 2 * SUBW + (si - 1) * SUBW
                mixer(seq, si, o, w, P0, si == 0)
                if si == 0:
                    S.op("dve", lambda e: e.tensor_copy(uhsv_t[:], uhist_t[:]), reads=[B_uh], writes=[B_sv])
                    S.op("dve", lambda e: e.tensor_copy(hssv_t[:], hst_t[:]), reads=[B_hst], writes=[B_sv])
            pf = Prefetch(*nxt) if (nxt is not None and USE_PF) else None
            hooks = {}
            if pf is not None:
                hooks = {"start": pf.start, "gu1": pf.norms}
            accs = ffn(1, fsubs, None, hooks)

            def do_out(accs=accs, fsubs=fsubs, seq=seq, m=m):
                for (si, o, w) in fsubs:
                    accs[si].to_out(seq, m * 2 * SUBW + (si - 1) * SUBW)
            prefetched = pf is not None
            prev_pf = pf
            prev_out = do_out
            if pf is None:
                do_out()
        S.finish()

        sems = {}
        for key in S.sem_keys():
            sems[key] = es.enter_context(nc.semaphore("s_" + "_".join(str(x) for x in key)))
        with nc.Block() as block:
            @block.tensor
            def _(e):
                S.emit("pe", e, sems)

            @block.scalar
            def _(e):
                S.emit("act", e, sems)

            @block.vector
            def _(e):
                S.emit("dve", e, sems)

            @block.gpsimd
            def _(e):
                S.emit("pool", e, sems)

            @block.sync
            def _(e):
                S.emit("sp", e, sems)
    return nc


def _consts(inp):
    f = lambda a: np.asarray(a, np.float32)
    c = np.zeros((128, NCONST), np.float32)

    def put(col, vec):
        v = f(vec).reshape(-1)
        n = v.size // 128
        c[:, col:col + n] = v.reshape(n, 128).T

    put(C_FFN1, inp["ffn1_norm"]); put(C_MIX, inp["mix_norm"]); put(C_FFN2, inp["ffn2_norm"]); put(C_FIN, inp["final_norm"])
    put(C_QLAT, inp["q_latent_norm"]); put(C_KVLAT, inp["kv_latent_norm"])
    perm = (np.arange(64) + 32) % 64
    for base, key in ((C_QN, "q_head_norm"), (C_KN, "k_head_norm")):
        g = f(inp[key]).reshape(-1)
        c[:, base] = g[0:128]
        c[0:64, base + 1] = g[128:192]
        c[0:64, base + 2] = g[128:192][perm]
    cw = f(inp["conv_w"]).reshape(4, 512)
    for j in range(4):
        put(C_CW + 4 * j, cw[j])
    put(C_CB, inp["conv_b"]); put(C_BA, inp["gate_a_b"]); put(C_BX, inp["gate_x_b"]); put(C_LAM, inp["lru_lambda"])
    put(C_AO, inp["attn_out_norm"]); put(C_LO, inp["lru_out_norm"])
    return c


def _rope_table():
    pos = np.arange(LTOT, dtype=np.float32)
    inv = (np.float32(10000.0) ** (-np.arange(0, 32, dtype=np.float32) / np.float32(32))).astype(np.float32)
    ang = (pos[None, :] * inv[:, None]).astype(np.float32)
    t = np.zeros((64, 2, LTOT), np.float32)
    t[0:32, 0] = np.cos(ang); t[32:64, 0] = np.cos(ang)
    t[0:32, 1] = np.sin(ang); t[32:64, 1] = np.sin(ang)
    return t


def make_in_maps(inp, ncores=8, nseq=2):
    f = lambda a: np.ascontiguousarray(np.asarray(a, np.float32))
    x = np.asarray(inp["x"], np.float32)
    shared = {
        "metaT": f(np.asarray(inp["meta_tokens"], np.float32).T),
        "consts": _consts(inp),
        "rope": _rope_table(),
        "ffn1_w_gate": f(inp["ffn1_w_gate"][0]), "ffn1_w_up": f(inp["ffn1_w_up"][0]), "ffn1_w_down": f(inp["ffn1_w_down"][0]),
        "ffn2_w_gate": f(inp["ffn2_w_gate"][0]), "ffn2_w_up": f(inp["ffn2_w_up"][0]), "ffn2_w_down": f(inp["ffn2_w_down"][0]),
        "w_in": f(inp["w_in"][0]), "w_uq": f(inp["w_uq"][0]), "w_uk": f(inp["w_uk"][0]), "w_uv": f(inp["w_uv"][0]),
        "gate_a_w": f(inp["gate_a_w"][0]), "gate_x_w": f(inp["gate_x_w"][0]), "w_out": f(inp["w_out"][0]),
    }
    maps = []
    for i in range(ncores):
        d = dict(shared)
        d["xT"] = f(np.transpose(x[i * nseq:(i + 1) * nseq], (0, 2, 1)))
        maps.append(d)
    return maps


def kernel(**inputs):
    nc = build(2, 2)
    maps = make_in_maps(inputs, 8, 2)
    res = run_bass_kernel_spmd(nc, maps, core_ids=list(range(8)))
    outs = [np.transpose(r["outT"], (0, 2, 1)) for r in res.results]
    return np.ascontiguousarray(np.concatenate(outs, axis=0).astype(np.float32))
```

```python
import bisect
import os
import types
from contextlib import ExitStack

import numpy as np
import concourse.bass as bass
import concourse.mybir as mybir
from concourse.bass_utils import run_bass_kernel_spmd

F32 = mybir.dt.float32
BF16 = mybir.dt.bfloat16
AF = mybir.ActivationFunctionType
ALU = mybir.AluOpType

D = 1024
DFF = 2816
NFC = 22
NFH = 11
SEQ = 2048
NMETA = 16
LTOT = SEQ + NMETA
NH = 4
EPS = 1e-6
SUBW = 512
MACW = NMETA + 2 * SUBW
GW = 520
NGRAN = 74
NKT = 17

C_FFN1 = 0; C_MIX = 8; C_FFN2 = 16; C_FIN = 24
C_QLAT = 32; C_KVLAT = 35
C_QN = 37; C_QR = 38; C_QRP = 39
C_KN = 40; C_KR = 41; C_KRP = 42
C_CW = 43
C_CB = 59; C_BA = 63; C_BX = 67; C_LAM = 71
C_AO = 75; C_LO = 79
NCONST = 83


class Buf:
    __slots__ = ("name", "w", "r", "rd", "x")

    def __init__(self, name, x=False):
        self.name = name
        self.x = x
        self.w = None
        self.r = {}
        self.rd = []


def _snap(fn):
    if fn is None or fn.__closure__ is None:
        return fn
    cells = []
    for c in fn.__closure__:
        try:
            v = c.cell_contents
        except ValueError:
            cells.append(c)
            continue
        if isinstance(v, types.FunctionType):
            v = _snap(v)
        cells.append(types.CellType(v))
    g = types.FunctionType(fn.__code__, fn.__globals__, fn.__name__, fn.__defaults__, tuple(cells))
    g.__kwdefaults__ = fn.__kwdefaults__
    return g


class Sched:
    ENG = ("pe", "act", "dve", "pool", "sp")
    KDMA = 8

    def __init__(self):
        self.ops = {e: [] for e in self.ENG}
        self.marks = {e: [] for e in self.ENG}
        self.waited = {e: {} for e in self.ENG}
        self.dma_n = {"sp": 0, "pool": 0, "act": 0}
        self.dma_cnt = {q: [0] * self.KDMA for q in self.dma_n}
        self.out_tokens = []

    def _resolve(self, tok):
        if tok[0] == "d":
            return ("d", tok[1], tok[2]), tok[3]
        eng, idx = tok[1], tok[2]
        marks = self.marks[eng]
        p = bisect.bisect_left(marks, idx)
        if p == len(marks):
            ops = self.ops[eng]
            j = len(ops) - 1
            while ops[j]["dmasem"] is not None or ops[j]["fn"] is None:
                j -= 1
            assert j >= idx
            ops[j]["mark"] = True
            marks.append(j)
            p = len(marks) - 1
        return ("e", eng), p + 1

    def _add_wait(self, eng, waits, tok):
        if tok is None:
            return
        if tok[0] == "e" and tok[1] == eng and eng == "pe":
            return
        key, val = self._resolve(tok)
        if self.waited[eng].get(key, 0) >= val:
            return
        self.waited[eng][key] = val
        waits.append((key, val))

    def op(self, eng, fn, reads=(), writes=(), dma_q=None):
        waits = []
        if any(b.x for b in reads):
            writes = list(writes) + [b for b in reads if b.x and b not in writes]
            reads = [b for b in reads if not b.x]
        for b in reads:
            self._add_wait(eng, waits, b.w)
        for b in writes:
            self._add_wait(eng, waits, b.w)
            for e2, i2 in b.r.items():
                self._add_wait(eng, waits, ("e", e2, i2))
            for t in b.rd:
                self._add_wait(eng, waits, t)
        idx = len(self.ops[eng])
        rec = {"waits": waits, "fn": _snap(fn), "mark": False, "dmasem": None}
        if dma_q is not None:
            q = eng
            k = self.dma_n[q] % self.KDMA
            self.dma_n[q] += 1
            if self.dma_cnt[q][k] > 0:
                self._add_wait(eng, waits, ("d", q, k, 16 * self.dma_cnt[q][k]))
            self.dma_cnt[q][k] += 1
            tok = ("d", q, k, 16 * self.dma_cnt[q][k])
            rec["dmasem"] = ("d", q, k)
        else:
            tok = ("e", eng, idx)
        self.ops[eng].append(rec)
        for b in writes:
            b.w = tok
            b.r = {}
            b.rd = []
        for b in reads:
            if b in writes:
                continue
            if tok[0] == "d":
                b.rd.append(tok)
            else:
                b.r[eng] = idx
        return tok

    def dma(self, q, out, in_, reads=(), writes=()):
        return self.op(q, lambda e: e.dma_start(out=out, in_=in_), reads, writes, dma_q=q)

    def finish(self):
        waits = []
        for t in self.out_tokens:
            self._add_wait("sp", waits, t)
        self.ops["sp"].append({"waits": waits, "fn": None, "mark": False, "dmasem": None})

    def sem_keys(self):
        keys = [("e", e) for e in self.ENG]
        for q in self.dma_n:
            for k in range(self.KDMA):
                keys.append(("d", q, k))
        return keys

    def emit(self, eng, e, sems):
        for rec in self.ops[eng]:
            for key, val in rec["waits"]:
                e.wait_ge(sems[key], val)
            if rec["fn"] is None:
                continue
            ins = rec["fn"](e)
            if rec["mark"]:
                ins.then_inc(sems[("e", eng)], 1)
            if rec["dmasem"] is not None:
                ins.then_inc(sems[rec["dmasem"]], 16)


class Slot:
    def __init__(self, kind, gi, ap, bufs):
        self.kind, self.gi, self.ap, self.bufs = kind, gi, ap, bufs


class Pool:
    def __init__(self, tensor, n):
        self.t = tensor
        self.n = n
        self.free = [True] * n
        self.bufs = [Buf(f"g{i}") for i in range(n)]

    def f(self):
        for i in range(0, self.n - 1, 2):
            if self.free[i] and self.free[i + 1]:
                self.free[i] = self.free[i + 1] = False
                ap = self.t[:, i * GW:(i + 2) * GW].bitcast(F32)
                return Slot("f", i, ap, [self.bufs[i], self.bufs[i + 1]])
        raise RuntimeError("pool exhausted (F)")

    def b(self):
        for i in range(self.n - 1, -1, -1):
            if self.free[i]:
                self.free[i] = False
                return Slot("b", i, self.t[:, i * GW:(i + 1) * GW], [self.bufs[i]])
        raise RuntimeError("pool exhausted (B)")

    def rel(self, *slots):
        for s in slots:
            if s.kind == "f":
                assert not self.free[s.gi] and not self.free[s.gi + 1]
                self.free[s.gi] = self.free[s.gi + 1] = True
            else:
                assert not self.free[s.gi]
                self.free[s.gi] = True


def build(nseq=2, nmacro=2, debug=(), stop=3):
    STOP = stop
    nc = bass.Bass("TRN2", target_bir_lowering=False)
    S = Sched()
    dbg_specs = {}

    def din(name, shape):
        return nc.dram_tensor(name, list(shape), F32, kind="ExternalInput").ap()

    xT = din("xT", [nseq, D, SEQ])
    metaT = din("metaT", [D, NMETA])
    consts_d = din("consts", [128, NCONST])
    rope_d = din("rope", [64, 2, LTOT])
    w_g = [din("ffn1_w_gate", [D, DFF]), din("ffn2_w_gate", [D, DFF])]
    w_u = [din("ffn1_w_up", [D, DFF]), din("ffn2_w_up", [D, DFF])]
    w_d = [din("ffn1_w_down", [DFF, D]), din("ffn2_w_down", [DFF, D])]
    w_in = din("w_in", [D, 1728])
    w_uq = din("w_uq", [384, 768])
    w_uk = din("w_uk", [256, 512])
    w_uv = din("w_uv", [256, 512])
    gate_a = din("gate_a_w", [8, 64, 64])
    gate_x = din("gate_x_w", [8, 64, 64])
    w_out = din("w_out", [D, D])
    outT = nc.dram_tensor("outT", [nseq, D, SEQ], F32, kind="ExternalOutput").ap()
    dbg_out = {}
    for name, shape in debug:
        dbg_out[name] = nc.dram_tensor("dbg_" + name, list(shape), F32, kind="ExternalOutput").ap()

    es = ExitStack()
    with es:
        def sb(name, shape, dt):
            return es.enter_context(nc.sbuf_tensor(name, list(shape), dt))

        h_t = sb("h", [128, 8, MACW], F32)
        xn_t = sb("xn", [128, 8, MACW], BF16)
        pool_t = sb("pool", [128, NGRAN * GW], BF16)
        aTm_t = sb("aTm", [128, NFH, NMETA], BF16)
        wring_t = sb("wring", [128, 6, 8, 128], BF16)
        wdring_t = sb("wdring", [128, 3, NFH, 128], BF16)
        kn_t = sb("kn", [128, NH, LTOT], BF16)
        kr_t = sb("kr", [64, NH, LTOT], BF16)
        v_t = sb("v", [128, NKT, 512], BF16)
        wuq_t = sb("wuq", [128, 3, 768], BF16)
        wuqrot_t = sb("wuqrot", [128, 3, 256], BF16)
        wuk_t = sb("wuk", [128, 2, 512], BF16)
        wuv_t = sb("wuv", [128, 2, 512], BF16)
        gat_t = sb("gat", [128, 2, 4, 128], BF16)
        cst_t = sb("cst", [128, NCONST], F32)
        der_t = sb("der", [128, 16], F32)
        ones_t = sb("ones", [128, 128], BF16)
        uhist_t = sb("uhist", [128, 4, 3], F32)
        hst_t = sb("hst", [128, 4], F32)
        uhsv_t = sb("uhsv", [128, 4, 3], F32)
        hssv_t = sb("hssv", [128, 4], F32)
        banks = [es.enter_context(nc.psum_tensor(f"ps{i}", [128, 512], F32)) for i in range(8)]

        pool = Pool(pool_t, NGRAN)
        B_h = [[Buf(f"h{c}_{s}") for s in range(3)] for c in range(8)]
        B_xn = [Buf(f"xn{s}") for s in range(3)]
        B_aTm = Buf("aTm")
        B_wr = [Buf(f"wr{i}") for i in range(6)]
        B_wd = [Buf(f"wd{i}") for i in range(3)]
        B_k = [Buf(f"k{i}") for i in range(NKT)]
        B_v = [Buf(f"v{i}") for i in range(NKT)]
        B_wq, B_wqr, B_wk, B_wv, B_gat = Buf("wq"), Buf("wqr"), Buf("wk"), Buf("wv"), Buf("gat")
        B_cst, B_der, B_ones, B_uh, B_hst = Buf("cst"), Buf("der"), Buf("ones"), Buf("uh"), Buf("hst")
        B_sv = Buf("sv")
        B_ps = [Buf(f"ps{i}", x=True) for i in range(8)]

        st = {"gen": 0, "acc": 0, "wr": 0, "wd": 0, "genset": [0, 1, 2, 3, 4, 5], "accset": [6, 7]}

        def ps_gen():
            gs = st["genset"]
            i = gs[st["gen"] % len(gs)]
            st["gen"] += 1
            return banks[i], B_ps[i]

        def ps_acc():
            a = st["accset"]
            i = a[st["acc"] % len(a)]
            st["acc"] += 1
            return banks[i], B_ps[i]

        def wr_next():
            i = st["wr"] % 6
            st["wr"] += 1
            return wring_t[:, i], B_wr[i]

        def wd_next():
            i = st["wd"] % 3
            st["wd"] += 1
            return wdring_t[:, i], B_wd[i]

        def cc(col, n=1, p=128):
            return cst_t[0:p, col:col + n]

        def tap(name, ap, reads, idx=None):
            if name in dbg_out:
                dst = dbg_out[name] if idx is None else dbg_out[name][idx]
                S.out_tokens.append(S.dma("sp", dst, ap, reads=reads))

        S.dma("sp", cst_t[:], consts_d, writes=[B_cst])
        S.op("dve", lambda e: e.memset(ones_t[:], 1.0), writes=[B_ones])
        S.op("dve", lambda e: e.memset(gat_t[:], 0.0), writes=[B_gat])
        SK = os.environ.get("K_SKIP", "")
        if "q" not in SK:
            S.dma("pool", wuq_t[:], w_uq.rearrange("(k p) n -> p k n", p=128), writes=[B_wq])
        S.dma("pool", wuk_t[:], w_uk.rearrange("(k p) n -> p k n", p=128), writes=[B_wk])
        S.dma("pool", wuv_t[:], w_uv.rearrange("(k p) n -> p k n", p=128), writes=[B_wv])
        for gi, gsrc in enumerate((gate_a, gate_x)):
            for n in range(8 if "g" not in SK else 0):
                o = 64 * (n % 2)
                S.dma("pool", gat_t[o:o + 64, gi, n // 2, o:o + 64], gsrc[n], writes=[B_gat])
        for hh in range(NH if "r" not in SK else 0):
            src0 = 192 * hh + 128
            S.op("dve", lambda e, hh=hh, src0=src0: e.tensor_scalar(
                wuqrot_t[:, :, 64 * hh:64 * hh + 32], wuq_t[:, :, src0 + 32:src0 + 64], -1.0, 0.0, ALU.mult, ALU.add),
                reads=[B_wq], writes=[B_wqr])
            S.op("dve", lambda e, hh=hh, src0=src0: e.tensor_copy(
                wuqrot_t[:, :, 64 * hh + 32:64 * hh + 64], wuq_t[:, :, src0:src0 + 32]),
                reads=[B_wq], writes=[B_wqr])
        S.op("dve", lambda e: e.tensor_scalar(der_t[:, 0:4], cc(C_BA, 4), 0.5, 0.0, ALU.mult, ALU.add),
             reads=[B_cst], writes=[B_der])
        S.op("dve", lambda e: e.tensor_scalar(der_t[:, 4:8], cc(C_BX, 4), 0.5, 0.0, ALU.mult, ALU.add),
             reads=[B_cst], writes=[B_der])
        S.op("act", lambda e: e.activation(out=der_t[:, 8:12], in_=cc(C_LAM, 4), func=AF.Exp, scale=-1.0),
             reads=[B_cst], writes=[B_der])
        S.op("act", lambda e: e.activation(out=der_t[:, 8:12], in_=der_t[:, 8:12], func=AF.Ln, bias=1.0, scale=1.0),
             reads=[B_der], writes=[B_der])
        S.op("dve", lambda e: e.tensor_scalar(der_t[:, 12:16], der_t[:, 8:12], -8.0, 0.0, ALU.mult, ALU.add),
             reads=[B_der], writes=[B_der])
        S.op("dve", lambda e: e.tensor_scalar(der_t[:, 8:12], der_t[:, 8:12], -4.0, 0.0, ALU.mult, ALU.add),
             reads=[B_der], writes=[B_der])

        def rstd_from(ssq_ps, ssq_b, pw, n, eps):
            r = pool.f()
            S.op("act", lambda e: e.activation(out=r.ap[:, 0:pw], in_=ssq_ps[:, 0:pw], func=AF.Ln,
                                               bias=float(eps), scale=1.0 / n),
                 reads=[ssq_b], writes=r.bufs)
            S.op("act", lambda e: e.activation(out=r.ap[:, 0:pw], in_=r.ap[:, 0:pw], func=AF.Exp, scale=-0.5),
                 reads=r.bufs, writes=r.bufs)
            return r

        def sq_accum(ssq_ps, ssq_b, sq_slot, p, pw, first, last):
            S.op("pe", lambda e: e.matmul(ssq_ps[:, 0:pw], ones_t[0:p, :], sq_slot.ap[0:p, 0:pw],
                                          start=first, stop=last),
                 reads=[B_ones] + sq_slot.bufs, writes=[ssq_b])

        class NormAcc:
            def __init__(self, si, o, w, bank):
                self.si, self.o, self.w = si, o, w
                self.ps, self.pb = banks[bank], B_ps[bank]
                self.n = 0
                self.pend = None

            def _flush(self, last):
                if self.pend is not None:
                    c, sq = self.pend
                    sq_accum(self.ps, self.pb, sq, 128, self.w, c == 0, last)
                    pool.rel(sq)
                    self.pend = None

            def add(self, c):
                o, w, si = self.o, self.w, self.si
                self._flush(False)
                sq = pool.b()
                S.op("act", lambda e: e.activation(out=sq.ap[:, 0:w], in_=h_t[:, c, o:o + w], func=AF.Square),
                     reads=[B_h[c][si]], writes=sq.bufs)
                self.pend = (self.n, sq)
                self.n += 1

            def rstd(self):
                assert self.n == 8
                self._flush(True)
                return rstd_from(self.ps, self.pb, self.w, D, EPS)

            def to_xn(self, gcol):
                o, w, si = self.o, self.w, self.si
                r = self.rstd()
                for c in range(8):
                    S.op("dve", lambda e, c=c: e.scalar_tensor_tensor(
                        xn_t[:, c, o:o + w], h_t[:, c, o:o + w], cc(gcol + c), r.ap[:, 0:w], ALU.mult, ALU.mult),
                        reads=[B_h[c][si], B_cst] + r.bufs, writes=[B_xn[si]])
                pool.rel(r)

            def to_out(self, seq, F0):
                o, w, si = self.o, self.w, self.si
                r = self.rstd()
                obs = [pool.f() for _ in range(4)]
                for c in range(8):
                    ob = obs[c % 4]
                    S.op("dve", lambda e, c=c, ob=ob: e.scalar_tensor_tensor(
                        ob.ap[:, 0:w], h_t[:, c, o:o + w], cc(C_FIN + c), r.ap[:, 0:w], ALU.mult, ALU.mult),
                        reads=[B_h[c][si], B_cst] + r.bufs, writes=ob.bufs)
                    S.out_tokens.append(S.dma("sp", outT[seq, c * 128:(c + 1) * 128, F0:F0 + w], ob.ap[:, 0:w], reads=ob.bufs))
                pool.rel(r, *obs)

        def norm_h(subs, gcol):
            for (si, o, w) in subs:
                ssq_ps, ssq_b = ps_acc()
                sqs = []
                for c in range(8):
                    sq = pool.b()
                    S.op("act", lambda e, c=c, sq=sq: e.activation(out=sq.ap[:, 0:w], in_=h_t[:, c, o:o + w], func=AF.Square),
                         reads=[B_h[c][si]], writes=sq.bufs)
                    sqs.append(sq)
                for c in range(8):
                    sq_accum(ssq_ps, ssq_b, sqs[c], 128, w, c == 0, c == 7)
                    pool.rel(sqs[c])
                r = rstd_from(ssq_ps, ssq_b, w, D, EPS)
                for c in range(8):
                    S.op("dve", lambda e, c=c: e.scalar_tensor_tensor(
                        xn_t[:, c, o:o + w], h_t[:, c, o:o + w], cc(gcol + c), r.ap[:, 0:w], ALU.mult, ALU.mult),
                        reads=[B_h[c][si], B_cst] + r.bufs, writes=[B_xn[si]])
                pool.rel(r)

        def ffn(k, subs, gcol, hooks=None):
            if gcol is not None:
                norm_h(subs, gcol)
            st["genset"] = [0, 1, 2, 3, 4]
            hooks = hooks or {}
            if "start" in hooks:
                hooks["start"]()
            accs = {si: NormAcc(si, o, w, 5 + si) for (si, o, w) in subs}
            for half in range(2):
                aT = {}
                for j in range(NFH):
                    f = half * NFH + j
                    wg_ap, wg_b = wr_next()
                    wu_ap, wu_b = wr_next()
                    S.dma("pool", wg_ap, w_g[k].rearrange("(c p) n -> p c n", p=128)[:, :, f * 128:(f + 1) * 128],
                          writes=[wg_b])
                    S.dma("pool", wu_ap, w_u[k].rearrange("(c p) n -> p c n", p=128)[:, :, f * 128:(f + 1) * 128],
                          writes=[wu_b])
                    for (si, o, w) in subs:
                        pg, pg_b = ps_gen()
                        pu, pu_b = ps_gen()

                        def mm(e, wt=wg_ap, ps=pg, o=o, w=w):
                            for c in range(8):
                                ins = e.matmul(ps[:, 0:w], wt[:, c, :], xn_t[:, c, o:o + w], start=(c == 0), stop=(c == 7))
                            return ins
                        S.op("pe", mm, reads=[wg_b, B_xn[si]], writes=[pg_b])
                        S.op("pe", lambda e, wt=wu_ap, ps=pu, o=o, w=w: mm(e, wt, ps, o, w),
                             reads=[wu_b, B_xn[si]], writes=[pu_b])
                        sg = pool.f()
                        S.op("act", lambda e, sg=sg, pg=pg, w=w: e.activation(out=sg.ap[:, 0:w], in_=pg[:, 0:w], func=AF.Silu),
                             reads=[pg_b], writes=sg.bufs)
                        if si == 0:
                            dst, dst_b = aTm_t[:, j, 0:w], [B_aTm]
                        else:
                            a = pool.b()
                            aT[(j, si)] = a
                            dst, dst_b = a.ap[:, 0:w], a.bufs
                        S.op("dve", lambda e, dst=dst, sg=sg, pu=pu, w=w: e.tensor_tensor(dst, sg.ap[:, 0:w], pu[:, 0:w], ALU.mult),
                             reads=sg.bufs + [pu_b], writes=dst_b)
                        pool.rel(sg)
                if ("gu%d" % half) in hooks:
                    hooks["gu%d" % half]()
                for c in range(8):
                    wd_ap, wd_b = wd_next()
                    S.dma("pool", wd_ap,
                          w_d[k][half * NFH * 128:(half + 1) * NFH * 128, c * 128:(c + 1) * 128].rearrange("(j p) n -> p j n", p=128),
                          writes=[wd_b])
                    for (si, o, w) in subs:
                        pd, pd_b = ps_gen()

                        def mmd(e, si=si, w=w, pd=pd, wd_ap=wd_ap):
                            for j in range(NFH):
                                rhs = aTm_t[:, j, 0:w] if si == 0 else aT[(j, si)].ap[:, 0:w]
                                ins = e.matmul(pd[:, 0:w], wd_ap[:, j, :], rhs, start=(j == 0), stop=(j == NFH - 1))
                            return ins
                        rb = [B_aTm] if si == 0 else [b for j in range(NFH) for b in aT[(j, si)].bufs]
                        S.op("pe", mmd, reads=[wd_b] + rb, writes=[pd_b])
                        S.op("dve", lambda e, c=c, o=o, w=w, pd=pd: e.scalar_tensor_tensor(
                            h_t[:, c, o:o + w], pd[:, 0:w], 0.5, h_t[:, c, o:o + w], ALU.mult, ALU.add),
                            reads=[pd_b, B_h[c][si]], writes=[B_h[c][si]])
                        if half == 1:
                            accs[si].add(c)
                for a in aT.values():
                    pool.rel(a)
            st["genset"] = [0, 1, 2, 3, 4, 5]
            return accs

        CQ = [(0, 128), (128, 128), (256, 128)]
        CKV = [(384, 128), (512, 128)]
        KRc = (640, 64)
        Uc = [(704 + 128 * c, 128) for c in range(4)]
        Gc = [(1216 + 128 * c, 128) for c in range(4)]

        def mixer(seq, si, o, w, P0, is_meta):
            LV = float(os.environ.get('K_MIX', '9'))
            st['genset'], st['accset'] = [0, 1, 2], [3, 4]
            if LV <= 0:
                return
            kt0 = 0 if is_meta else 1 + (P0 - NMETA) // 128
            nkb = 1 if is_meta else w // 128
            winr = w_in.rearrange("(c p) n -> p c n", p=128)

            def win_group(col0, M, negrot=False):
                wt, wb = wr_next()
                if not negrot:
                    S.dma("pool", wt[:, :, 0:M], winr[:, :, col0:col0 + M], writes=[wb])
                else:
                    S.dma("pool", wt[:, :, 32:64], winr[:, :, col0:col0 + 32], writes=[wb])
                    S.dma("pool", wt[:, :, 0:32], winr[:, :, col0 + 32:col0 + 64], writes=[wb])
                    S.op("dve", lambda e: e.tensor_scalar(wt[:, :, 0:32], wt[:, :, 0:32], -1.0, 0.0, ALU.mult, ALU.add),
                         reads=[wb], writes=[wb])
                ps, pb = ps_gen()

                def mm(e):
                    for c in range(8):
                        ins = e.matmul(ps[0:M, 0:w], wt[:, c, 0:M], xn_t[:, c, o:o + w], start=(c == 0), stop=(c == 7))
                    return ins
                S.op("pe", mm, reads=[wb, B_xn[si]], writes=[pb])
                return ps, pb

            def raw_and_sq(ps, pb, p, need_raw=True):
                sq = pool.b()
                S.op("act", lambda e: e.activation(out=sq.ap[0:p, 0:w], in_=ps[0:p, 0:w], func=AF.Square),
                     reads=[pb], writes=sq.bufs)
                raw = None
                if need_raw:
                    raw = pool.f()
                    S.op("dve", lambda e: e.tensor_copy(raw.ap[0:p, 0:w], ps[0:p, 0:w]), reads=[pb], writes=raw.bufs)
                return raw, sq

            def latent(blocks, gcol, n):
                ssq_ps, ssq_b = ps_acc()
                raws, pend = [], None
                for i, (c0, M) in enumerate(blocks):
                    ps, pb = win_group(c0, M)
                    raw, sq = raw_and_sq(ps, pb, 128)
                    raws.append(raw)
                    if pend is not None:
                        sq_accum(ssq_ps, ssq_b, pend[1], 128, w, pend[0] == 0, False)
                        pool.rel(pend[1])
                    pend = (i, sq)
                sq_accum(ssq_ps, ssq_b, pend[1], 128, w, pend[0] == 0, True)
                pool.rel(pend[1])
                r = rstd_from(ssq_ps, ssq_b, w, n, EPS)
                outs = []
                for i, raw in enumerate(raws):
                    ob = pool.b()
                    S.op("dve", lambda e, i=i, raw=raw, ob=ob: e.scalar_tensor_tensor(
                        ob.ap[:, 0:w], raw.ap[:, 0:w], cc(gcol + i), r.ap[:, 0:w], ALU.mult, ALU.mult),
                        reads=raw.bufs + r.bufs + [B_cst], writes=ob.bufs)
                    pool.rel(raw)
                    outs.append(ob)
                pool.rel(r)
                return outs

            cosS, sinS = pool.f(), pool.f()
            S.dma("sp", cosS.ap[0:64, 0:w], rope_d[:, 0, P0:P0 + w], writes=cosS.bufs)
            S.dma("sp", sinS.ap[0:64, 0:w], rope_d[:, 1, P0:P0 + w], writes=sinS.bufs)

            def roped(ps_r, pb_r, ps_rot, pb_rot, gcol_r, gcol_rp):
                t1, t2 = pool.f(), pool.f()
                S.op("dve", lambda e: e.scalar_tensor_tensor(t1.ap[0:64, 0:w], ps_r[0:64, 0:w], cc(gcol_r, 1, 64),
                                                             cosS.ap[0:64, 0:w], ALU.mult, ALU.mult),
                     reads=[pb_r, B_cst] + cosS.bufs, writes=t1.bufs)
                S.op("dve", lambda e: e.scalar_tensor_tensor(t2.ap[0:64, 0:w], ps_rot[0:64, 0:w], cc(gcol_rp, 1, 64),
                                                             sinS.ap[0:64, 0:w], ALU.mult, ALU.mult),
                     reads=[pb_rot, B_cst] + sinS.bufs, writes=t2.bufs)
                S.op("dve", lambda e: e.tensor_tensor(t1.ap[0:64, 0:w], t1.ap[0:64, 0:w], t2.ap[0:64, 0:w], ALU.add),
                     reads=t1.bufs + t2.bufs, writes=t1.bufs)
                pool.rel(t2)
                return t1

            if LV <= 0.5:
                pool.free = [True] * pool.n
                return
            cqn = None
            if not is_meta:
                cqn = latent(CQ, C_QLAT, 384)
            ckvn = latent(CKV, C_KVLAT, 256)

            if LV <= 1:
                pool.free = [True] * pool.n
                return
            ps_kr, pb_kr = win_group(KRc[0], 64)
            ps_krot, pb_krot = win_group(KRc[0], 64, negrot=True)
            _, sq_kr = raw_and_sq(ps_kr, pb_kr, 64, need_raw=False)
            kroped = roped(ps_kr, pb_kr, ps_krot, pb_krot, C_KR, C_KRP)

            if LV <= 2:
                pool.free = [True] * pool.n
                return
            ub, graw = [], []
            for c in range(4):
                ps, pb = win_group(*Uc[c])
                u = pool.f()
                S.op("act", lambda e, u=u, ps=ps: e.activation(out=u.ap[:, 3:3 + w], in_=ps[:, 0:w], func=AF.Copy),
                     reads=[pb], writes=u.bufs)
                S.op("dve", lambda e, u=u, c=c: e.tensor_copy(u.ap[:, 0:3], uhist_t[:, c, :]), reads=[B_uh], writes=u.bufs)
                ub.append(u)
                if not is_meta:
                    ps, pb = win_group(*Gc[c])
                    g = pool.f()
                    S.op("act", lambda e, g=g, ps=ps: e.activation(out=g.ap[:, 0:w], in_=ps[:, 0:w], func=AF.Copy),
                         reads=[pb], writes=g.bufs)
                    graw.append(g)
            for c in range(4):
                S.op("dve", lambda e, c=c: e.tensor_copy(uhist_t[:, c, :], ub[c].ap[:, w:w + 3]),
                     reads=ub[c].bufs, writes=[B_uh])

            yln = []
            lb = {"i": 0}

            def lru_bank():
                i = 5 + lb["i"] % 2
                lb["i"] += 1
                return banks[i], B_ps[i]

            def YOP(*a, **k):
                return S.op(*a, **k)

            def lru_gen():
                yl = [None] * 4
                for pair in ((0, 1), (2, 3)):
                    T = {}
                    for c in pair:
                        u = ub[c]
                        xc = pool.f()
                        T[c] = {"xc": xc}
                        yield YOP("dve", lambda e, xc=xc, u=u, c=c: e.tensor_scalar(xc.ap[:, 0:w], u.ap[:, 0:w], cc(C_CW + c), cc(C_CB + c), ALU.mult, ALU.add),
                             reads=u.bufs + [B_cst], writes=xc.bufs)
                    for j in range(1, 4):
                        for c in pair:
                            u, xc = ub[c], T[c]["xc"]
                            yield YOP("dve", lambda e, xc=xc, u=u, c=c, j=j: e.scalar_tensor_tensor(
                                xc.ap[:, 0:w], u.ap[:, j:j + w], cc(C_CW + 4 * j + c), xc.ap[:, 0:w], ALU.mult, ALU.add),
                                reads=u.bufs + xc.bufs + [B_cst], writes=xc.bufs)
                    for c in pair:
                        pool.rel(ub[c])
                        xc = T[c]["xc"]
                        xcb = pool.b()
                        T[c]["xcb"] = xcb
                        yield YOP("dve", lambda e, xcb=xcb, xc=xc: e.tensor_copy(xcb.ap[:, 0:w], xc.ap[:, 0:w]), reads=xc.bufs, writes=xcb.bufs)
                    for c in pair:
                        xcb = T[c]["xcb"]
                        pa, pa_b = lru_bank()
                        px, px_b = lru_bank()
                        YOP("pe", lambda e, pa=pa, c=c, xcb=xcb: e.matmul(pa[:, 0:w], gat_t[:, 0, c, :], xcb.ap[:, 0:w], start=True, stop=True),
                            reads=[B_gat] + xcb.bufs, writes=[pa_b])
                        YOP("pe", lambda e, px=px, c=c, xcb=xcb: e.matmul(px[:, 0:w], gat_t[:, 1, c, :], xcb.ap[:, 0:w], start=True, stop=True),
                            reads=[B_gat] + xcb.bufs, writes=[px_b])
                        pool.rel(xcb)
                        tr, ti = pool.f(), pool.f()
                        T[c]["tr"], T[c]["ti"] = tr, ti
                        YOP("act", lambda e, tr=tr, pa=pa, c=c: e.activation(out=tr.ap[:, 0:w], in_=pa[:, 0:w], func=AF.Tanh, bias=der_t[:, c:c + 1], scale=0.5),
                            reads=[pa_b, B_der], writes=tr.bufs)
                        yield YOP("act", lambda e, ti=ti, px=px, c=c: e.activation(out=ti.ap[:, 0:w], in_=px[:, 0:w], func=AF.Tanh, bias=der_t[:, 4 + c:5 + c], scale=0.5),
                             reads=[px_b, B_der], writes=ti.bufs)
                    for c in pair:
                        tr = T[c]["tr"]
                        a1, a2 = pool.f(), pool.f()
                        T[c]["a1"], T[c]["a2"] = a1, a2
                        yield YOP("act", lambda e, a1=a1, tr=tr, c=c: e.activation(out=a1.ap[:, 0:w], in_=tr.ap[:, 0:w], func=AF.Exp,
                                                                               bias=der_t[:, 8 + c:9 + c], scale=der_t[:, 8 + c:9 + c]),
                             reads=tr.bufs + [B_der], writes=a1.bufs)
                        yield YOP("act", lambda e, a2=a2, tr=tr, c=c: e.activation(out=a2.ap[:, 0:w], in_=tr.ap[:, 0:w], func=AF.Exp,
                                                                               bias=der_t[:, 12 + c:13 + c], scale=der_t[:, 12 + c:13 + c]),
                             reads=tr.bufs + [B_der], writes=a2.bufs)
                        pool.rel(tr)
                    for c in pair:
                        a2 = T[c]["a2"]
                        yield YOP("act", lambda e, a2=a2: e.activation(out=a2.ap[:, 0:w], in_=a2.ap[:, 0:w], func=AF.Ln, bias=1.0, scale=-1.0),
                             reads=a2.bufs, writes=a2.bufs)
                    for c in pair:
                        a2 = T[c]["a2"]
                        yield YOP("act", lambda e, a2=a2: e.activation(out=a2.ap[:, 0:w], in_=a2.ap[:, 0:w], func=AF.Exp, scale=0.5),
                             reads=a2.bufs, writes=a2.bufs)
                    for c in pair:
                        ti, xc = T[c]["ti"], T[c]["xc"]
                        yield YOP("dve", lambda e, ti=ti, xc=xc: e.scalar_tensor_tensor(ti.ap[:, 0:w], ti.ap[:, 0:w], 1.0, xc.ap[:, 0:w], ALU.add, ALU.mult),
                             reads=ti.bufs + xc.bufs, writes=ti.bufs)
                        pool.rel(xc)
                    for c in pair:
                        ti, a2 = T[c]["ti"], T[c]["a2"]
                        yield YOP("dve", lambda e, a2=a2, ti=ti: e.scalar_tensor_tensor(a2.ap[:, 0:w], a2.ap[:, 0:w], 0.5, ti.ap[:, 0:w], ALU.mult, ALU.mult),
                             reads=a2.bufs + ti.bufs, writes=a2.bufs)
                        if is_meta:
                            yield YOP("dve", lambda e, a2=a2, ti=ti: e.tensor_scalar(a2.ap[:, 0:1], ti.ap[:, 0:1], 0.5, 0.0, ALU.mult, ALU.add),
                                 reads=a2.bufs + ti.bufs, writes=a2.bufs)
                    for c in pair:
                        hl, a1, a2 = T[c]["ti"], T[c]["a1"], T[c]["a2"]
                        init = 0.0 if is_meta else hst_t[:, c:c + 1]
                        yield YOP("dve", lambda e, hl=hl, a1=a1, a2=a2, init=init: e.tensor_tensor_scan(hl.ap[:, 0:w], a1.ap[:, 0:w], a2.ap[:, 0:w], init, ALU.mult, ALU.add),
                             reads=a1.bufs + a2.bufs + [B_hst], writes=hl.bufs)
                    for c in pair:
                        hl = T[c]["ti"]
                        yield YOP("dve", lambda e, hl=hl, c=c: e.tensor_copy(hst_t[:, c:c + 1], hl.ap[:, w - 1:w]), reads=hl.bufs, writes=[B_hst])
                        pool.rel(T[c]["a1"], T[c]["a2"])
                        if is_meta:
                            pool.rel(hl)
                    if is_meta:
                        continue
                    for c in pair:
                        g = graw[c]
                        g2 = pool.f()
                        T[c]["g2"] = g2
                        yield YOP("act", lambda e, g2=g2, g=g: e.activation(out=g2.ap[:, 0:w], in_=g.ap[:, 0:w], func=AF.Square),
                             reads=g.bufs, writes=g2.bufs)
                    for c in pair:
                        g2 = T[c]["g2"]
                        yield YOP("dve", lambda e, g2=g2: e.tensor_scalar(g2.ap[:, 0:w], g2.ap[:, 0:w], 0.044715, 1.0, ALU.mult, ALU.add),
                             reads=g2.bufs, writes=g2.bufs)
                    for c in pair:
                        g, g2 = graw[c], T[c]["g2"]
                        yield YOP("dve", lambda e, g2=g2, g=g: e.tensor_tensor(g2.ap[:, 0:w], g2.ap[:, 0:w], g.ap[:, 0:w], ALU.mult),
                             reads=g2.bufs + g.bufs, writes=g2.bufs)
                    for c in pair:
                        g2 = T[c]["g2"]
                        yield YOP("act", lambda e, g2=g2: e.activation(out=g2.ap[:, 0:w], in_=g2.ap[:, 0:w], func=AF.Tanh, scale=0.7978845608028654),
                             reads=g2.bufs, writes=g2.bufs)
                    for c in pair:
                        g, g2 = graw[c], T[c]["g2"]
                        yield YOP("dve", lambda e, g2=g2, g=g: e.scalar_tensor_tensor(g2.ap[:, 0:w], g2.ap[:, 0:w], 1.0, g.ap[:, 0:w], ALU.add, ALU.mult),
                             reads=g2.bufs + g.bufs, writes=g2.bufs)
                    for c in pair:
                        g, g2, hl = graw[c], T[c]["g2"], T[c]["ti"]
                        yield YOP("dve", lambda e, g2=g2, hl=hl: e.scalar_tensor_tensor(hl.ap[:, 0:w], g2.ap[:, 0:w], 0.5, hl.ap[:, 0:w], ALU.mult, ALU.mult),
                             reads=g2.bufs + hl.bufs, writes=hl.bufs)
                        pool.rel(g2, g)
                        yl[c] = hl
                if is_meta:
                    return
                yield
                tap("ylru", yl[0].ap[:, 0:w], yl[0].bufs)
                ssq_ps, ssq_b = banks[7], B_ps[7]
                for c in range(4):
                    sq = pool.b()
                    yield YOP("act", lambda e, sq=sq, c=c: e.activation(out=sq.ap[:, 0:w], in_=yl[c].ap[:, 0:w], func=AF.Square),
                         reads=yl[c].bufs, writes=sq.bufs)
                    yield sq_accum(ssq_ps, ssq_b, sq, 128, w, c == 0, c == 3)
                    pool.rel(sq)
                r = rstd_from(ssq_ps, ssq_b, w, 512, EPS)
                yield
                for c in range(4):
                    ob = pool.b()
                    yield YOP("dve", lambda e, ob=ob, c=c, r=r: e.scalar_tensor_tensor(
                        ob.ap[:, 0:w], yl[c].ap[:, 0:w], cc(C_LO + c), r.ap[:, 0:w], ALU.mult, ALU.mult),
                        reads=yl[c].bufs + r.bufs + [B_cst], writes=ob.bufs)
                    yln.append(ob)
                pool.rel(r, *yl)


            lru_it = lru_gen()

            def lru_step(n):
                for _ in range(n):
                    try:
                        next(lru_it)
                    except StopIteration:
                        return

            if LV <= 3:
                pool.free = [True] * pool.n
                return
            kbufs = [B_k[kt0 + i] for i in range(nkb)]
            st['genset'] = [0, 1, 2, 3, 4]
            kps, ksq, knrs = [], [], []
            for hh in range(NH):
                ps, pb = ps_gen()

                def mmk(e, ps=ps, hh=hh):
                    for kc in range(2):
                        ins = e.matmul(ps[:, 0:w], wuk_t[:, kc, hh * 128:(hh + 1) * 128], ckvn[kc].ap[:, 0:w],
                                       start=(kc == 0), stop=(kc == 1))
                    return ins
                S.op("pe", mmk, reads=[B_wk] + ckvn[0].bufs + ckvn[1].bufs, writes=[pb])
                kps.append((ps, pb))
            for hh in range(NH):
                ps, pb = kps[hh]
                sq = pool.b()
                S.op("act", lambda e, sq=sq, ps=ps: e.activation(out=sq.ap[:, 0:w], in_=ps[:, 0:w], func=AF.Square),
                     reads=[pb], writes=sq.bufs)
                knr = pool.f()
                S.op("dve", lambda e, knr=knr, ps=ps: e.tensor_scalar(knr.ap[:, 0:w], ps[:, 0:w], cc(C_KN), None, ALU.mult),
                     reads=[pb, B_cst], writes=knr.bufs)
                ksq.append(sq)
                knrs.append(knr)
            lru_step(6)
            kss = []
            for hh in range(NH):
                ps, pb = ps_gen()

                def mmss(e, ps=ps, sq=ksq[hh]):
                    e.matmul(ps[:, 0:w], ones_t[:, :], sq.ap[:, 0:w], start=True, stop=False)
                    return e.matmul(ps[:, 0:w], ones_t[0:64, :], sq_kr.ap[0:64, 0:w], start=False, stop=True)
                S.op("pe", mmss, reads=[B_ones] + ksq[hh].bufs + sq_kr.bufs, writes=[pb])
                kss.append((ps, pb))
            krs = []
            for hh in range(NH):
                ps, pb = kss[hh]
                r = pool.f()
                S.op("act", lambda e, r=r, ps=ps: e.activation(out=r.ap[:, 0:w], in_=ps[:, 0:w], func=AF.Ln, bias=float(EPS), scale=1.0 / 192),
                     reads=[pb], writes=r.bufs)
                krs.append(r)
            for hh in range(NH):
                r = krs[hh]
                S.op("act", lambda e, r=r: e.activation(out=r.ap[:, 0:w], in_=r.ap[:, 0:w], func=AF.Exp, scale=-0.5),
                     reads=r.bufs, writes=r.bufs)
            lru_step(6)
            for hh in range(NH):
                r, knr = krs[hh], knrs[hh]
                S.op("dve", lambda e, hh=hh, knr=knr, r=r: e.tensor_tensor(kn_t[:, hh, P0:P0 + w], knr.ap[:, 0:w], r.ap[:, 0:w], ALU.mult),
                     reads=knr.bufs + r.bufs, writes=kbufs)
                S.op("dve", lambda e, hh=hh, r=r: e.tensor_tensor(kr_t[:, hh, P0:P0 + w], kroped.ap[0:64, 0:w], r.ap[0:64, 0:w], ALU.mult),
                     reads=kroped.bufs + r.bufs, writes=kbufs)
                pool.rel(knr, r, ksq[hh])
            pool.rel(sq_kr, kroped)
            for tb in range(nkb):
                nt = NMETA if is_meta else 128
                ps, pb = ps_gen()

                def mmv(e, ps=ps, tb=tb, nt=nt):
                    for kc in range(2):
                        ins = e.matmul(ps[0:nt, :], ckvn[kc].ap[:, tb * 128:tb * 128 + nt], wuv_t[:, kc, :],
                                       start=(kc == 0), stop=(kc == 1))
                    return ins
                S.op("pe", mmv, reads=[B_wv] + ckvn[0].bufs + ckvn[1].bufs, writes=[pb])
                S.op("act", lambda e, ps=ps, tb=tb, nt=nt: e.activation(out=v_t[0:nt, kt0 + tb, :], in_=ps[0:nt, :], func=AF.Copy),
                     reads=[pb], writes=[B_v[kt0 + tb]])
            pool.rel(*ckvn)
            st['genset'] = [0, 1, 2]

            if LV <= 4:
                pool.free = [True] * pool.n
                return
            ymn = []
            if not is_meta:
                Qn, Qr = [None] * NH, [None] * NH
                rq = [b for s_ in cqn for b in s_.bufs]
                for batch in ((0, 1), (2, 3)):
                    held = {}
                    for hh in batch:
                        psn, pbn = ps_gen()
                        psr, pbr = ps_gen()
                        pso, pbo = ps_gen()

                        def mmq(e, ps, M, lhs):
                            for kc in range(3):
                                ins = e.matmul(ps[0:M, 0:w], lhs(kc), cqn[kc].ap[:, 0:w], start=(kc == 0), stop=(kc == 2))
                            return ins
                        S.op("pe", lambda e, hh=hh, psn=psn: mmq(e, psn, 128, lambda kc: wuq_t[:, kc, 192 * hh:192 * hh + 128]),
                             reads=[B_wq] + rq, writes=[pbn])
                        S.op("pe", lambda e, hh=hh, psr=psr: mmq(e, psr, 64, lambda kc: wuq_t[:, kc, 192 * hh + 128:192 * hh + 192]),
                             reads=[B_wq] + rq, writes=[pbr])
                        S.op("pe", lambda e, hh=hh, pso=pso: mmq(e, pso, 64, lambda kc: wuqrot_t[:, kc, 64 * hh:64 * hh + 64]),
                             reads=[B_wqr] + rq, writes=[pbo])
                        sqn, sqr = pool.b(), pool.b()
                        S.op("act", lambda e, sqn=sqn, psn=psn: e.activation(out=sqn.ap[:, 0:w], in_=psn[:, 0:w], func=AF.Square),
                             reads=[pbn], writes=sqn.bufs)
                        S.op("act", lambda e, sqr=sqr, psr=psr: e.activation(out=sqr.ap[0:64, 0:w], in_=psr[0:64, 0:w], func=AF.Square),
                             reads=[pbr], writes=sqr.bufs)
                        qnr = pool.f()
                        S.op("dve", lambda e, qnr=qnr, psn=psn: e.tensor_scalar(qnr.ap[:, 0:w], psn[:, 0:w], cc(C_QN), None, ALU.mult),
                             reads=[pbn, B_cst], writes=qnr.bufs)
                        qrp = roped(psr, pbr, pso, pbo, C_QR, C_QRP)
                        held[hh] = (sqn, sqr, qnr, qrp)
                    lru_step(4)
                    sss = {}
                    for hh in batch:
                        sqn, sqr, qnr, qrp = held[hh]
                        ps, pb = ps_acc()

                        def mmss(e, ps=ps, sqn=sqn, sqr=sqr):
                            e.matmul(ps[:, 0:w], ones_t[:, :], sqn.ap[:, 0:w], start=True, stop=False)
                            return e.matmul(ps[:, 0:w], ones_t[0:64, :], sqr.ap[0:64, 0:w], start=False, stop=True)
                        S.op("pe", mmss, reads=[B_ones] + sqn.bufs + sqr.bufs, writes=[pb])
                        sss[hh] = (ps, pb)
                    rs = {}
                    for hh in batch:
                        ps, pb = sss[hh]
                        r = pool.f()
                        S.op("act", lambda e, r=r, ps=ps: e.activation(out=r.ap[:, 0:w], in_=ps[:, 0:w], func=AF.Ln, bias=float(EPS), scale=1.0 / 192),
                             reads=[pb], writes=r.bufs)
                        rs[hh] = r
                    for hh in batch:
                        r = rs[hh]
                        S.op("act", lambda e, r=r: e.activation(out=r.ap[:, 0:w], in_=r.ap[:, 0:w], func=AF.Exp, scale=-0.5),
                             reads=r.bufs, writes=r.bufs)
                    for hh in batch:
                        sqn, sqr, qnr, qrp = held[hh]
                        r = rs[hh]
                        qn_b, qr_b = pool.b(), pool.b()
                        S.op("dve", lambda e, qn_b=qn_b, qnr=qnr, r=r: e.tensor_tensor(qn_b.ap[:, 0:w], qnr.ap[:, 0:w], r.ap[:, 0:w], ALU.mult),
                             reads=qnr.bufs + r.bufs, writes=qn_b.bufs)
                        S.op("dve", lambda e, qr_b=qr_b, qrp=qrp, r=r: e.tensor_tensor(qr_b.ap[0:64, 0:w], qrp.ap[0:64, 0:w], r.ap[0:64, 0:w], ALU.mult),
                             reads=qrp.bufs + r.bufs, writes=qr_b.bufs)
                        pool.rel(sqn, sqr, qnr, qrp, r)
                        Qn[hh], Qr[hh] = qn_b, qr_b
                    lru_step(4)
                pool.rel(*cqn)
                pool.rel(cosS, sinS)

                if LV <= 5:
                    pool.free = [True] * pool.n
                    return
                nqt = w // 128
                kt_last = kt0 + nqt - 1
                sc = 1.0 / float(np.sqrt(192.0))
                ymla = []
                ssq_mla, ssq_mla_b = None, None
                for hh in range(NH):
                    po, po_b = banks[3], B_ps[3]
                    prs, prs_b = banks[4], B_ps[4]

                    def kinfo(kt):
                        if kt == 0:
                            return 0, NMETA, 0
                        lq = max(0, kt - kt0)
                        return NMETA + (kt - 1) * 128, 128, lq * 128

                    def emit_s(kt, hh=hh):
                        kp, nk, q0 = kinfo(kt)
                        i = st["s4"] % 3
                        st["s4"] += 1
                        ps, pb = banks[i], B_ps[i]

                        def mms(e):
                            e.matmul(ps[0:nk, q0:w], kn_t[:, hh, kp:kp + nk], Qn[hh].ap[:, q0:w], start=True, stop=False)
                            return e.matmul(ps[0:nk, q0:w], kr_t[0:64, hh, kp:kp + nk], Qr[hh].ap[0:64, q0:w], start=False, stop=True)
                        S.op("pe", mms, reads=[B_k[kt]] + Qn[hh].bufs + Qr[hh].bufs, writes=[pb])
                        pt = pool.b()
                        S.op("act", lambda e: e.activation(out=pt.ap[0:nk, q0:w], in_=ps[0:nk, q0:w], func=AF.Exp, scale=sc),
                             reads=[pb], writes=pt.bufs)
                        if kt >= kt0 and kt > 0:
                            S.op("dve", lambda e: e.memset(pt.ap[64:128, q0:q0 + 64], 0.0), writes=pt.bufs)
                        return pt

                    def emit_pv(kt, pt, hh=hh, po=po, po_b=po_b, prs=prs, prs_b=prs_b):
                        kp, nk, q0 = kinfo(kt)
                        S.op("pe", lambda e: e.matmul(po[:, q0:w], v_t[0:nk, kt, hh * 128:(hh + 1) * 128], pt.ap[0:nk, q0:w],
                                                      start=(kt == 0), stop=(kt == kt_last), skip_group_check=True),
                             reads=[B_v[kt]] + pt.bufs, writes=[po_b])
                        S.op("pe", lambda e: e.matmul(prs[:, q0:w], ones_t[0:nk, :], pt.ap[0:nk, q0:w],
                                                      start=(kt == 0), stop=(kt == kt_last), skip_group_check=True),
                             reads=[B_ones] + pt.bufs, writes=[prs_b])
                        pool.rel(pt)

                    st.setdefault("s4", 0)
                    pts = {0: emit_s(0)}
                    if kt_last >= 1:
                        pts[1] = emit_s(1)
                    for kt in range(0, kt_last + 1):
                        if kt + 2 <= kt_last:
                            pts[kt + 2] = emit_s(kt + 2)
                        emit_pv(kt, pts.pop(kt))
                        lru_step(2)
                    rc = pool.f()
                    S.op("dve", lambda e, rc=rc, prs=prs: e.reciprocal(rc.ap[:, 0:w], prs[:, 0:w]), reads=[prs_b], writes=rc.bufs)
                    y = pool.f()
                    S.op("dve", lambda e, y=y, rc=rc, po=po: e.tensor_tensor(y.ap[:, 0:w], po[:, 0:w], rc.ap[:, 0:w], ALU.mult),
                         reads=[po_b] + rc.bufs, writes=y.bufs)
                    pool.rel(rc)
                    ymla.append(y)
                for s_ in Qn + Qr:
                    pool.rel(s_)
                tap("ymla", ymla[0].ap[:, 0:w], ymla[0].bufs)
                ssq_ps, ssq_b = ps_acc()
                for hh in range(NH):
                    sq = pool.b()
                    S.op("act", lambda e, sq=sq, hh=hh: e.activation(out=sq.ap[:, 0:w], in_=ymla[hh].ap[:, 0:w], func=AF.Square),
                         reads=ymla[hh].bufs, writes=sq.bufs)
                    sq_accum(ssq_ps, ssq_b, sq, 128, w, hh == 0, hh == NH - 1)
                    pool.rel(sq)
                r = rstd_from(ssq_ps, ssq_b, w, 512, EPS)
                for hh in range(NH):
                    ob = pool.b()
                    S.op("dve", lambda e, ob=ob, hh=hh, r=r: e.scalar_tensor_tensor(
                        ob.ap[:, 0:w], ymla[hh].ap[:, 0:w], cc(C_AO + hh), r.ap[:, 0:w], ALU.mult, ALU.mult),
                        reads=ymla[hh].bufs + r.bufs + [B_cst], writes=ob.bufs)
                    ymn.append(ob)
                pool.rel(r, *ymla)
            else:
                pool.rel(cosS, sinS)

            if LV <= 6:
                pool.free = [True] * pool.n
                return
            lru_step(100000)
            if is_meta:
                st['genset'], st['accset'] = [0, 1, 2, 3, 4, 5], [6, 7]
                return
            ymn = ymn + yln
            acc2 = NormAcc(si, o, w, 5)
            woutr = w_out.rearrange("(c p) n -> p c n", p=128)
            for c in range(8):
                wt, wb = wr_next()
                S.dma("pool", wt, woutr[:, :, c * 128:(c + 1) * 128], writes=[wb])
                ps, pb = ps_gen()

                def mmo(e, ps=ps, wt=wt):
                    for kc in range(8):
                        ins = e.matmul(ps[:, 0:w], wt[:, kc, :], ymn[kc].ap[:, 0:w], start=(kc == 0), stop=(kc == 7))
                    return ins
                S.op("pe", mmo, reads=[wb] + [b for s_ in ymn for b in s_.bufs], writes=[pb])
                S.op("dve", lambda e, c=c, ps=ps: e.tensor_tensor(h_t[:, c, o:o + w], ps[:, 0:w], h_t[:, c, o:o + w], ALU.add),
                     reads=[pb, B_h[c][si]], writes=[B_h[c][si]])
                if STOP >= 3:
                    acc2.add(c)
            pool.rel(*ymn)
            if STOP >= 3:
                acc2.to_xn(C_FFN2)
            st['genset'], st['accset'] = [0, 1, 2, 3, 4, 5], [6, 7]

        def final_store(seq, si, o, w, F0):
            ssq_ps, ssq_b = ps_acc()
            sqs = []
            for c in range(8):
                sq = pool.b()
                S.op("act", lambda e, c=c, sq=sq: e.activation(out=sq.ap[:, 0:w], in_=h_t[:, c, o:o + w], func=AF.Square),
                     reads=[B_h[c][si]], writes=sq.bufs)
                sqs.append(sq)
            for c in range(8):
                sq_accum(ssq_ps, ssq_b, sqs[c], 128, w, c == 0, c == 7)
                pool.rel(sqs[c])
            r = rstd_from(ssq_ps, ssq_b, w, D, EPS)
            for c in range(8):
                ob = pool.f()
                S.op("dve", lambda e, c=c, ob=ob: e.scalar_tensor_tensor(
                    ob.ap[:, 0:w], h_t[:, c, o:o + w], cc(C_FIN + c), r.ap[:, 0:w], ALU.mult, ALU.mult),
                    reads=[B_h[c][si], B_cst] + r.bufs, writes=ob.bufs)
                S.out_tokens.append(S.dma("sp", outT[seq, c * 128:(c + 1) * 128, F0:F0 + w], ob.ap[:, 0:w], reads=ob.bufs))
                pool.rel(ob)
            pool.rel(r)

        def load_direct(seq, m, si):
            if si == 0:
                for c in range(8):
                    S.dma("sp", h_t[:, c, 0:NMETA], metaT[c * 128:(c + 1) * 128, :], writes=[B_h[c][0]])
            else:
                o = NMETA + (si - 1) * SUBW
                F0 = m * 2 * SUBW + (si - 1) * SUBW
                for c in range(8):
                    S.dma("sp", h_t[:, c, o:o + SUBW], xT[seq, c * 128:(c + 1) * 128, F0:F0 + SUBW], writes=[B_h[c][si]])

        class Prefetch:
            def __init__(self, seq, m):
                self.seq, self.m = seq, m
                self.stg = {}
                self.sq = {}

            def start(self):
                for si in (1, 2):
                    F0 = self.m * 2 * SUBW + (si - 1) * SUBW
                    self.stg[si] = [pool.f() for _ in range(8)]
                    for c in range(8):
                        sl = self.stg[si][c]
                        S.dma("sp", sl.ap[:, 0:SUBW], xT[self.seq, c * 128:(c + 1) * 128, F0:F0 + SUBW], writes=sl.bufs)

            def squares(self, sis=(1, 2)):
                for si in sis:
                    self.sq[si] = []
                    for c in range(8):
                        sl = self.stg[si][c]
                        sq = pool.b()
                        S.op("act", lambda e, sl=sl, sq=sq: e.activation(out=sq.ap[:, 0:SUBW], in_=sl.ap[:, 0:SUBW], func=AF.Square),
                             reads=sl.bufs, writes=sq.bufs)
                        self.sq[si].append(sq)

            def norms(self):
                if False:
                    sqs = []
                    for c in range(8):
                        sq = pool.b()
                        S.op("act", lambda e, c=c, sq=sq: e.activation(out=sq.ap[:, 0:NMETA], in_=h_t[:, c, 0:NMETA], func=AF.Square),
                             reads=[B_h[c][0]], writes=sq.bufs)
                        sqs.append(sq)
                    ps, pb = banks[5], B_ps[5]

                    def mm0(e, sqs=sqs, ps=ps):
                        for i, sq in enumerate(sqs):
                            ins = e.matmul(ps[:, 0:NMETA], ones_t[:, :], sq.ap[:, 0:NMETA], start=(i == 0), stop=(i == 7))
                        return ins
                    S.op("pe", mm0, reads=[B_ones] + [b for sq in sqs for b in sq.bufs], writes=[pb])
                    pool.rel(*sqs)
                    r = rstd_from(ps, pb, NMETA, D, EPS)
                    for c in range(8):
                        S.op("dve", lambda e, c=c, r=r: e.scalar_tensor_tensor(
                            xn_t[:, c, 0:NMETA], h_t[:, c, 0:NMETA], cc(C_FFN1 + c), r.ap[:, 0:NMETA], ALU.mult, ALU.mult),
                            reads=[B_h[c][0], B_cst] + r.bufs, writes=[B_xn[0]])
                    pool.rel(r)
                for si in (1, 2):
                    self.squares((si,))
                    o = NMETA + (si - 1) * SUBW
                    ps, pb = banks[5], B_ps[5]
                    sqs = self.sq[si]

                    def mm(e, sqs=sqs, ps=ps):
                        for i, sq in enumerate(sqs):
                            ins = e.matmul(ps[:, 0:SUBW], ones_t[:, :], sq.ap[:, 0:SUBW], start=(i == 0), stop=(i == 7))
                        return ins
                    S.op("pe", mm, reads=[B_ones] + [b for sq in sqs for b in sq.bufs], writes=[pb])
                    pool.rel(*sqs)
                    r = rstd_from(ps, pb, SUBW, D, EPS)
                    for c in range(8):
                        sl = self.stg[si][c]
                        S.op("dve", lambda e, c=c, sl=sl, o=o, r=r: e.scalar_tensor_tensor(
                            xn_t[:, c, o:o + SUBW], sl.ap[:, 0:SUBW], cc(C_FFN1 + c), r.ap[:, 0:SUBW], ALU.mult, ALU.mult),
                            reads=sl.bufs + [B_cst] + r.bufs, writes=[B_xn[si]])
                    pool.rel(r)

            def land(self):
                for si in (1, 2):
                    o = NMETA + (si - 1) * SUBW
                    for c in range(8):
                        sl = self.stg[si][c]
                        S.op("act", lambda e, c=c, sl=sl, o=o: e.activation(out=h_t[:, c, o:o + SUBW], in_=sl.ap[:, 0:SUBW], func=AF.Copy),
                             reads=sl.bufs, writes=[B_h[c][si]])
                        pool.rel(sl)

        macros = [(seq, m) for seq in range(nseq) for m in range(nmacro)]
        USE_PF = os.environ.get("K_PF", "1") == "1"
        prefetched = False
        for mi, (seq, m) in enumerate(macros):
            nxt = macros[mi + 1] if mi + 1 < len(macros) else None
            if m == 0 and seq == 0:
                S.op("dve", lambda e: e.memset(uhist_t[:], 0.0), writes=[B_uh])
                S.op("dve", lambda e: e.memset(hst_t[:], 0.0), writes=[B_hst])
            elif m == 0:
                S.op("dve", lambda e: e.tensor_copy(uhist_t[:], uhsv_t[:]), reads=[B_sv], writes=[B_uh])
                S.op("dve", lambda e: e.tensor_copy(hst_t[:], hssv_t[:]), reads=[B_sv], writes=[B_hst])
            subs = []
            if m == 0 and seq == 0:
                subs.append((0, 0, NMETA))
            subs += [(1, NMETA, SUBW), (2, NMETA + SUBW, SUBW)]
            fsubs = [s_ for s_ in subs if s_[0] != 0]
            if not prefetched:
                for (si, o, w) in subs:
                    load_direct(seq, m, si)
                accs = ffn(0, subs, C_FFN1)
            else:
                def deferred(prev_out=prev_out, prev_pf=prev_pf):
                    prev_out()
                    prev_pf.land()
                accs = ffn(0, subs, None, {"gu0": deferred})
            for (si, o, w) in subs:
                accs[si].to_xn(C_MIX)
            for (si, o, w) in subs:
                P0 = 0 if si == 0 else NMETA + m * 2 * SUBW + (si - 1) * SUBW
                mixer(seq, si, o, w, P0, si == 0)
                if si == 0:
                    S.op("dve", lambda e: e.tensor_copy(uhsv_t[:], uhist_t[:]), reads=[B_uh], writes=[B_sv])
                    S.op("dve", lambda e: e.tensor_copy(hssv_t[:], hst_t[:]), reads=[B_hst], writes=[B_sv])
            pf = Prefetch(*nxt) if (nxt is not None and USE_PF) else None
            hooks = {}
            if pf is not None:
                hooks = {"start": pf.start, "gu1": pf.norms}
            accs = ffn(1, fsubs, None, hooks)

            def do_out(accs=accs, fsubs=fsubs, seq=seq, m=m):
                for (si, o, w) in fsubs:
                    accs[si].to_out(seq, m * 2 * SUBW + (si - 1) * SUBW)
            prefetched = pf is not None
            prev_pf = pf
            prev_out = do_out
            if pf is None:
                do_out()
        S.finish()

        sems = {}
        for key in S.sem_keys():
            sems[key] = es.enter_context(nc.semaphore("s_" + "_".join(str(x) for x in key)))
        with nc.Block() as block:
            @block.tensor
            def _(e):
                S.emit("pe", e, sems)

            @block.scalar
            def _(e):
                S.emit("act", e, sems)

            @block.vector
            def _(e):
                S.emit("dve", e, sems)

            @block.gpsimd
            def _(e):
                S.emit("pool", e, sems)

            @block.sync
            def _(e):
                S.emit("sp", e, sems)
    return nc


def _consts(inp):
    f = lambda a: np.asarray(a, np.float32)
    c = np.zeros((128, NCONST), np.float32)

    def put(col, vec):
        v = f(vec).reshape(-1)
        n = v.size // 128
        c[:, col:col + n] = v.reshape(n, 128).T

    put(C_FFN1, inp["ffn1_norm"]); put(C_MIX, inp["mix_norm"]); put(C_FFN2, inp["ffn2_norm"]); put(C_FIN, inp["final_norm"])
    put(C_QLAT, inp["q_latent_norm"]); put(C_KVLAT, inp["kv_latent_norm"])
    perm = (np.arange(64) + 32) % 64
    for base, key in ((C_QN, "q_head_norm"), (C_KN, "k_head_norm")):
        g = f(inp[key]).reshape(-1)
        c[:, base] = g[0:128]
        c[0:64, base + 1] = g[128:192]
        c[0:64, base + 2] = g[128:192][perm]
    cw = f(inp["conv_w"]).reshape(4, 512)
    for j in range(4):
        put(C_CW + 4 * j, cw[j])
    put(C_CB, inp["conv_b"]); put(C_BA, inp["gate_a_b"]); put(C_BX, inp["gate_x_b"]); put(C_LAM, inp["lru_lambda"])
    put(C_AO, inp["attn_out_norm"]); put(C_LO, inp["lru_out_norm"])
    return c


def _rope_table():
    pos = np.arange(LTOT, dtype=np.float32)
    inv = (np.float32(10000.0) ** (-np.arange(0, 32, dtype=np.float32) / np.float32(32))).astype(np.float32)
    ang = (pos[None, :] * inv[:, None]).astype(np.float32)
    t = np.zeros((64, 2, LTOT), np.float32)
    t[0:32, 0] = np.cos(ang); t[32:64, 0] = np.cos(ang)
    t[0:32, 1] = np.sin(ang); t[32:64, 1] = np.sin(ang)
    return t


def make_in_maps(inp, ncores=8, nseq=2):
    f = lambda a: np.ascontiguousarray(np.asarray(a, np.float32))
    x = np.asarray(inp["x"], np.float32)
    shared = {
        "metaT": f(np.asarray(inp["meta_tokens"], np.float32).T),
        "consts": _consts(inp),
        "rope": _rope_table(),
        "ffn1_w_gate": f(inp["ffn1_w_gate"][0]), "ffn1_w_up": f(inp["ffn1_w_up"][0]), "ffn1_w_down": f(inp["ffn1_w_down"][0]),
        "ffn2_w_gate": f(inp["ffn2_w_gate"][0]), "ffn2_w_up": f(inp["ffn2_w_up"][0]), "ffn2_w_down": f(inp["ffn2_w_down"][0]),
        "w_in": f(inp["w_in"][0]), "w_uq": f(inp["w_uq"][0]), "w_uk": f(inp["w_uk"][0]), "w_uv": f(inp["w_uv"][0]),
        "gate_a_w": f(inp["gate_a_w"][0]), "gate_x_w": f(inp["gate_x_w"][0]), "w_out": f(inp["w_out"][0]),
    }
    maps = []
    for i in range(ncores):
        d = dict(shared)
        d["xT"] = f(np.transpose(x[i * nseq:(i + 1) * nseq], (0, 2, 1)))
        maps.append(d)
    return maps


def kernel(**inputs):
    nc = build(2, 2)
    maps = make_in_maps(inputs, 8, 2)
    res = run_bass_kernel_spmd(nc, maps, core_ids=list(range(8)))
    outs = [np.transpose(r["outT"], (0, 2, 1)) for r in res.results]
    return np.ascontiguousarray(np.concatenate(outs, axis=0).astype(np.float32))
```
